# Optimizing a Trainium2 kernel written in Bass

```python
import math
import jax, jax.numpy as jnp
from jax import lax
import numpy as np

D_MODEL = 1024
BATCH = 8
SEQ = 2048
DEPTH = 4

HG_HEADS = 4
HG_DK = 128
HG_DV = 128
HG_K = HG_HEADS * HG_DK
HG_V = HG_HEADS * HG_DV
HG_CHUNK = 64
SG_GROUPS = 4
SG_WIDTH = 512
SG_GROUP_DIM = SG_WIDTH // SG_GROUPS
SG_CHUNK = 128
D_FF = 2816
ALPHA = (2 * DEPTH) ** 0.25
BETA = (8 * DEPTH) ** -0.25
LN_EPS = 1e-5
RMS_EPS = 1e-6
D_IN = 2 * HG_K + 2 * HG_V + 2 * SG_WIDTH + 2 * D_MODEL
IN_SPLITS = (HG_K,
             2 * HG_K,
             2 * HG_K + HG_V,
             2 * HG_K + 2 * HG_V,
             2 * HG_K + 2 * HG_V + SG_WIDTH,
             2 * HG_K + 2 * HG_V + 2 * SG_WIDTH,
             2 * HG_K + 2 * HG_V + 2 * SG_WIDTH + D_MODEL)

kernel_name = "hgrn2_sgu_macaron_deepnorm_hybrid"


def layer_norm(x, g, b):
    xf = x.astype(jnp.float32)
    mu = jnp.mean(xf, axis=-1, keepdims=True)
    var = jnp.mean(jnp.square(xf - mu), axis=-1, keepdims=True)
    y = (xf - mu) * lax.rsqrt(var + LN_EPS) * g.astype(jnp.float32) + b.astype(jnp.float32)
    return y.astype(x.dtype)


def swiglu(x, w13, w2):
    a, b = jnp.split(x @ w13, 2, axis=-1)
    return (jax.nn.silu(a) * b) @ w2


def hgrn2_chunkwise(q, k, i, log_f):
    B, T, H, _ = q.shape
    n_chunks = T // HG_CHUNK

    def to_chunks(a):
        return a.reshape(B, n_chunks, HG_CHUNK, H, a.shape[-1]).transpose(1, 0, 3, 2, 4)

    causal = jnp.tril(jnp.ones((HG_CHUNK, HG_CHUNK), dtype=bool))[:, :, None]

    def step(S, inp):
        q_c, k_c, i_c, lf_c = inp
        b = jnp.cumsum(lf_c, axis=-2)
        o_inter = jnp.einsum('bhtk,bhkv->bhtv', q_c * jnp.exp(b), S)
        diff = b[:, :, :, None, :] - b[:, :, None, :, :]
        decay = jnp.where(causal, jnp.exp(jnp.where(causal, diff, 0.0)), 0.0)
        scores = jnp.einsum('bhtk,bhtsk,bhsk->bhts', q_c, decay, k_c)
        o_intra = jnp.einsum('bhts,bhsv->bhtv', scores, i_c)
        b_last = b[:, :, -1:, :]
        S_new = jnp.exp(b_last[:, :, 0, :])[..., None] * S + jnp.einsum(
            'bhsk,bhsv->bhkv', k_c * jnp.exp(b_last - b), i_c)
        return S_new, o_inter + o_intra

    S0 = jnp.zeros((B, H, q.shape[-1], i.shape[-1]), jnp.float32)
    _, o = lax.scan(step, S0, (to_chunks(q), to_chunks(k), to_chunks(i), to_chunks(log_f)))
    return o.transpose(1, 0, 3, 2, 4).reshape(B, T, H, i.shape[-1])


def mixer(x, w_in, lb, hg_norm_g, sg_ln_g, sg_ln_b, sg_ws, sg_bs, w_branch_a, w_branch_b, w_out):
    B, T, _ = x.shape
    zq, zf, zi, zg, zu, zv, za, zb = jnp.split(x @ w_in, IN_SPLITS, axis=-1)

    f32 = jnp.float32
    lbf = lb.astype(f32)
    zff = zf.astype(f32)
    f = lbf + (1.0 - lbf) * jax.nn.sigmoid(zff)
    log_f = jnp.log(f)
    k = (1.0 - lbf) * jax.nn.sigmoid(-zff)
    hs = lambda a, d: a.reshape(B, T, HG_HEADS, d)
    o = hgrn2_chunkwise(hs(zq.astype(f32), HG_DK), hs(k, HG_DK),
                        hs(zi.astype(f32), HG_DV), hs(log_f, HG_DK))
    o = o * lax.rsqrt(jnp.mean(jnp.square(o), axis=-1, keepdims=True) + RMS_EPS)
    o = o * hg_norm_g.astype(f32).reshape(HG_HEADS, HG_DV)
    o = o.reshape(B, T, HG_V).astype(x.dtype) * jax.nn.silu(zg)
    y_a = o @ w_branch_a

    u = jax.nn.gelu(zu)
    v = layer_norm(jax.nn.gelu(zv), sg_ln_g, sg_ln_b)
    n_chunks = T // SG_CHUNK
    v = v.reshape(B, n_chunks, SG_CHUNK, SG_GROUPS, SG_GROUP_DIM)
    ws = sg_ws * jnp.tril(jnp.ones((SG_CHUNK, SG_CHUNK), sg_ws.dtype))
    mixed = jnp.einsum('gts,bnsgc->bntgc', ws, v) + sg_bs.T[None, None, :, :, None]
    y_b = (u * mixed.reshape(B, T, SG_WIDTH)) @ w_branch_b

    y = jax.nn.sigmoid(za) * y_a + jax.nn.sigmoid(zb) * y_b
    return y @ w_out


def setup_inputs(seed: int = 0) -> dict:
    key = jax.random.key(seed)
    ks = jax.random.split(key, 20)
    n = jax.random.normal
    f32 = jnp.float32
    return {
        "x": n(ks[0], (BATCH, SEQ, D_MODEL), f32),
        "ffn1_w13": n(ks[1], (DEPTH, D_MODEL, 2 * D_FF), f32) * D_MODEL ** -0.5,
        "ffn1_w2": n(ks[2], (DEPTH, D_FF, D_MODEL), f32) * (D_FF ** -0.5 * BETA),
        "ffn2_w13": n(ks[3], (DEPTH, D_MODEL, 2 * D_FF), f32) * D_MODEL ** -0.5,
        "ffn2_w2": n(ks[4], (DEPTH, D_FF, D_MODEL), f32) * (D_FF ** -0.5 * BETA),
        "ln_g": 1.0 + 0.02 * n(ks[5], (DEPTH, 3, D_MODEL), f32),
        "ln_b": 0.02 * n(ks[6], (DEPTH, 3, D_MODEL), f32),
        "w_in": n(ks[7], (DEPTH, D_MODEL, D_IN), f32) * D_MODEL ** -0.5,
        "hg_lb": 0.5 * n(ks[8], (DEPTH, HG_K), f32),
        "hg_norm_g": 1.0 + 0.02 * n(ks[9], (DEPTH, HG_V), f32),
        "sg_ln_g": 1.0 + 0.02 * n(ks[10], (DEPTH, SG_WIDTH), f32),
        "sg_ln_b": 0.02 * n(ks[11], (DEPTH, SG_WIDTH), f32),
        "sg_ws": n(ks[12], (DEPTH, SG_GROUPS, SG_CHUNK, SG_CHUNK), f32) * SG_CHUNK ** -0.5,
        "sg_bs": 1.0 + 0.1 * n(ks[13], (DEPTH, SG_GROUPS, SG_CHUNK), f32),
        "w_branch_a": n(ks[14], (DEPTH, HG_V, D_MODEL), f32) * (HG_V ** -0.5 * BETA),
        "w_branch_b": n(ks[15], (DEPTH, SG_WIDTH, D_MODEL), f32) * (SG_WIDTH ** -0.5 * BETA),
        "w_out": n(ks[16], (DEPTH, D_MODEL, D_MODEL), f32) * (D_MODEL ** -0.5 * BETA),
    }


def reference(x, ffn1_w13, ffn1_w2, ffn2_w13, ffn2_w2, ln_g, ln_b, w_in, hg_lb, hg_norm_g,
              sg_ln_g, sg_ln_b, sg_ws, sg_bs, w_branch_a, w_branch_b, w_out):
    lb_soft = jax.nn.softmax(hg_lb.astype(jnp.float32), axis=0)
    lower_bounds = jnp.cumsum(lb_soft, axis=0) - lb_soft[0]
    for l in range(DEPTH):
        x = layer_norm(ALPHA * x + 0.5 * swiglu(x, ffn1_w13[l], ffn1_w2[l]), ln_g[l, 0], ln_b[l, 0])
        m = mixer(x, w_in[l], lower_bounds[l], hg_norm_g[l], sg_ln_g[l], sg_ln_b[l], sg_ws[l],
                  sg_bs[l], w_branch_a[l], w_branch_b[l], w_out[l])
        x = layer_norm(ALPHA * x + m, ln_g[l, 1], ln_b[l, 1])
        x = layer_norm(ALPHA * x + 0.5 * swiglu(x, ffn2_w13[l], ffn2_w2[l]), ln_g[l, 2], ln_b[l, 2])
    return x
```

```python
import contextlib
import numpy as np
import concourse.bass as bass
import concourse.mybir as mybir
from concourse.bass_utils import run_bass_kernel_spmd

F32 = mybir.dt.float32
BF16 = mybir.dt.bfloat16
AF = mybir.ActivationFunctionType
ALU = mybir.AluOpType

D = 1024
KC = 8
DFF = 2816
NJ = 22
DEPTH = 4
NT = 1024
TB = 512
DIN = 5120
ALPHA = (2 * DEPTH) ** 0.25
LN_EPS = 1e-5
RMS_EPS = 1e-6
GC1 = 0.7978845608028654
GC2 = 0.044715
COMPUTE = ("pe", "act", "dve", "pool")
STAGES = 3


class Sched:
    def __init__(self, nc):
        self.nc = nc
        self.ops = []
        self.lastw = {}
        self.rd_eng = {}
        self.rd_dma = {}

    def op(self, eng, fn, r=(), w=(), dma=False, semkey=None):
        idx = len(self.ops)
        deps = set()
        for k in r:
            if k in self.lastw:
                deps.add(self.lastw[k])
        for k in w:
            if k in self.lastw:
                deps.add(self.lastw[k])
            deps.update(self.rd_eng.get(k, {}).values())
            deps.update(self.rd_dma.get(k, ()))
        for k in r:
            if dma:
                self.rd_dma.setdefault(k, []).append(idx)
            else:
                self.rd_eng.setdefault(k, {})[eng] = idx
        for k in w:
            self.lastw[k] = idx
            self.rd_eng[k] = {}
            self.rd_dma[k] = []
        if dma and semkey is None:
            semkey = w[0]
        self.ops.append(dict(eng=eng, fn=fn, deps=deps, dma=dma, semkey=semkey))
        return idx

    def emit(self):
        nc = self.nc
        ops = self.ops
        for i, o in enumerate(ops):
            best = {}
            dd = []
            for d in o["deps"]:
                p = ops[d]
                if p["dma"]:
                    dd.append(d)
                else:
                    if p["eng"] == "pe" and o["eng"] == "pe" and not o["dma"]:
                        continue
                    if d > best.get(p["eng"], -1):
                        best[p["eng"]] = d
            o["cdeps"] = list(best.values()) + dd
        needs = set()
        for o in ops:
            needs.update(o["cdeps"])
        stack = contextlib.ExitStack()
        eng_sem = {e: stack.enter_context(nc.semaphore("s_" + e)) for e in COMPUTE}
        dma_sem = {}
        eng_cnt = {e: 0 for e in COMPUTE}
        dma_cnt = {}
        for i, o in enumerate(ops):
            if o["dma"]:
                k = o["semkey"]
                if k not in dma_sem:
                    dma_sem[k] = stack.enter_context(nc.semaphore("d%d" % len(dma_sem)))
                    dma_cnt[k] = 0
                dma_cnt[k] += 16
                o["sig"] = (dma_sem[k], dma_cnt[k])
            elif i in needs:
                eng_cnt[o["eng"]] += 1
                o["sig"] = (eng_sem[o["eng"]], eng_cnt[o["eng"]])
            else:
                o["sig"] = None
        streams = {}
        for i, o in enumerate(ops):
            streams.setdefault(o["eng"], []).append(i)

        def run(ename, e):
            waited = {}
            for i in streams.get(ename, []):
                o = ops[i]
                for d in o["cdeps"]:
                    sem, val = ops[d]["sig"]
                    key = id(sem)
                    if waited.get(key, 0) >= val:
                        continue
                    e.wait_ge(sem, val)
                    waited[key] = val
                ins = o["fn"](e)
                if o["sig"] is not None:
                    ins.then_inc(o["sig"][0], 16 if o["dma"] else 1)
            last = {}
            for i in streams.get(ename, []):
                o = ops[i]
                if o["dma"]:
                    last[id(o["sig"][0])] = o["sig"]
            for sem, val in last.values():
                if waited.get(id(sem), 0) < val:
                    e.wait_ge(sem, val)

        with nc.Block() as block:
            @block.tensor
            def _(e):
                run("pe", e)

            @block.scalar
            def _(e):
                run("act", e)

            @block.vector
            def _(e):
                run("dve", e)

            @block.gpsimd
            def _(e):
                run("pool", e)

            @block.sync
            def _(e):
                run("sp", e)
        stack.close()


def make_consts():
    s = np.arange(128)[:, None]
    t = np.arange(128)[None, :]
    same = (s // 64) == (t // 64)
    ident = np.eye(128, dtype=np.float32)
    ublk = ((s <= t) & same).astype(np.float32)
    mgt = ((s > t) & same).astype(np.float32)
    triu = (s <= t).astype(np.float32)
    c = np.zeros((128, 10, 128), np.float32)
    c[:, 0] = ident
    c[:, 1] = mgt
    for i in range(4):
        c[:, 2 + i] = ublk
        c[:, 6 + i] = triu
    return c


def build_program(depth=DEPTH, nhalf=2, debug=False):
    nc = bass.Bass("TRN2", target_bir_lowering=False)
    T = nhalf * NT
    dr = lambda name, shape, kind="ExternalInput": nc.dram_tensor(name, list(shape), F32, kind=kind).ap()
    x_d = dr("x", [T, D])
    w13_d = [dr("ffn1_w13", [DEPTH, D, 2 * DFF]), dr("ffn2_w13", [DEPTH, D, 2 * DFF])]
    w2_d = [dr("ffn1_w2", [DEPTH, DFF, D]), dr("ffn2_w2", [DEPTH, DFF, D])]
    lng_d = dr("ln_g", [DEPTH * 3, D])
    lnb_d = dr("ln_b", [DEPTH * 3, D])
    win_d = dr("w_in", [DEPTH, D, DIN])
    hglb_d = dr("hg_lb", [DEPTH, 512])
    hgn_d = dr("hg_norm_g", [DEPTH, 512])
    sglg_d = dr("sg_ln_g", [DEPTH, 512])
    sglb_d = dr("sg_ln_b", [DEPTH, 512])
    sgws_d = dr("sg_ws", [DEPTH, 4, 128, 128])
    sgbs_d = dr("sg_bs", [DEPTH, 512])
    wa_d = dr("w_branch_a", [DEPTH, 512, D])
    wbb_d = dr("w_branch_b", [DEPTH, 512, D])
    wo_d = dr("w_out", [DEPTH, D, D])
    cst_d = dr("consts", [128, 10, 128])
    out_d = dr("out", [T, D], kind="ExternalOutput")
    dbg_d = dr("dbg", [3, 128, KC, NT], kind="ExternalOutput") if debug else None

    es = contextlib.ExitStack()
    sb = lambda name, shape, dt: es.enter_context(nc.sbuf_tensor(name, list(shape), dt))
    S = Sched(nc)
    op = S.op

    xa = sb("xa", [128, KC, NT], F32)
    xb = sb("xb", [128, KC, NT], BF16)
    scr = sb("scr", [128, NJ * NT], BF16)
    wk = [sb("wk%d" % i, [128, KC, 256], BF16) for i in range(8)]
    w2b = [sb("w2b%d" % i, [128, NJ, 128], BF16) for i in range(2)]
    tmp = sb("tmp", [128, 6, 512], F32)
    tbf = sb("tbf", [128, 4, 512], BF16)
    cst = sb("cst", [128, 10, 128], F32)
    ones_bf = sb("ones_bf", [128, 128], BF16)
    ones_f = sb("ones_f", [1, 128], F32)
    epsc = sb("epsc", [128, 2], F32)
    colraw = sb("colraw", [128, 128], F32)
    gcol = sb("gcol", [128, 96], F32)
    bcol = sb("bcol", [128, 96], F32)
    agcol = sb("agcol", [128, 96], F32)
    abcol = sb("abcol", [128, 96], F32)
    hcol = sb("hcol", [128, 32], F32)
    lbc = sb("lbc", [128, 4, 4, 4], F32)
    Sst = sb("Sst", [128, depth, 4, 128], F32)
    Smb = sb("Smb", [128, 2, 4, 128], BF16)
    rows = sb("rows", [128, 4, 512], F32)
    bsrow = sb("bsrow", [1, 512], F32)
    wsT = sb("wsT", [128, 4, 128], BF16)
    qtT = sb("qtT", [128, 4, 512], BF16)
    ktT = sb("ktT", [128, 4, 512], BF16)
    small = sb("small", [128, 64], F32)
    lnst = sb("lnst", [128, 3, 512], F32)
    PTs = sb("PTs", [128, 2, 512], BF16)
    psum = [es.enter_context(nc.psum_tensor("ps%d" % i, [128, 512], F32)) for i in range(8)]
    PK = lambda b: ("ps", b)

    def scr_bf(off_blk, nblk):
        return scr[:, off_blk * 512:(off_blk + nblk) * 512], [("scr", b) for b in range(off_blk, off_blk + nblk)]

    def scr_f32(off_blk, nblk):
        ap, keys = scr_bf(off_blk, nblk)
        return ap.bitcast(F32), keys

    def hT(j, tb):
        return scr[:, j * NT + tb * TB: j * NT + (tb + 1) * TB], [("scr", j * 2 + tb)]

    sgT_ap, sgT_k = scr_f32(0, 8)
    guT_ap, guT_k = scr_f32(8, 8)
    gv_ap, gv_k = scr_f32(16, 8)
    lf_ap, lf_k = scr_f32(24, 8)
    vn_ap, vn_k = scr_bf(32, 4)
    kh_ap, kh_k = scr_bf(36, 4)
    it_ap, it_k = scr_bf(40, 4)
    v3 = lambda ap: ap.rearrange("p (a n) -> p a n", a=4)
    sgT, guT, gv, lf, vn, kh, it = map(v3, (sgT_ap, guT_ap, gv_ap, lf_ap, vn_ap, kh_ap, it_ap))
    yT_ap, yT_k = scr_bf(16, 8)
    hb_ap, hb_k = scr_bf(24, 4)
    og_ap, og_k = scr_bf(28, 4)
    yT = yT_ap.rearrange("p (a n) -> p a n", a=8)
    hbT = v3(hb_ap)
    ogT = v3(og_ap)

    bankrot = {"a": [0, [0, 1, 2, 3]], "b": [0, [4, 5]]}

    def bank(g):
        st = bankrot[g]
        b = st[1][st[0] % len(st[1])]
        st[0] += 1
        return b

    wkrot = [0]

    def wk_next():
        i = wkrot[0] % 8
        wkrot[0] += 1
        return i

    def load_wk(i, src, view=None):
        dst = wk[i][:] if view is None else view(wk[i])
        op("pool", lambda e: e.dma_start(out=dst, in_=src), w=[("wk", i)], dma=True)

    tmprot = [0]

    def tmp_next():
        i = tmprot[0] % 6
        tmprot[0] += 1
        return i

    tbfrot = [0]

    def tbf_next():
        i = tbfrot[0] % 4
        tbfrot[0] += 1
        return i

    def act(out, in_, func, r, w, **kw):
        op("act", lambda e: e.activation(out=out, in_=in_, func=func, **kw), r=r, w=w)

    def tt(out, in0, in1, alu, r, w, eng="dve"):
        op(eng, lambda e: e.tensor_tensor(out=out, in0=in0, in1=in1, op=alu), r=r, w=w)

    def ts(out, in0, s1, s2, op0, op1, r, w):
        if s2 is None:
            op("dve", lambda e: e.tensor_scalar(out=out, in0=in0, scalar1=s1, scalar2=None, op0=op0), r=r, w=w)
        else:
            op("dve", lambda e: e.tensor_scalar(out=out, in0=in0, scalar1=s1, scalar2=s2, op0=op0, op1=op1), r=r, w=w)

    def stt(out, in0, scalar, in1, op0, op1, r, w):
        op("dve", lambda e: e.scalar_tensor_tensor(out=out, in0=in0, scalar=scalar, in1=in1, op0=op0, op1=op1), r=r, w=w)

    def mm(out, lhsT, rhs, start, stop, r, w, sgc=False):
        if sgc:
            op("pe", lambda e: e.matmul(out, lhsT, rhs, start=start, stop=stop, skip_group_check=True), r=r, w=w)
        else:
            op("pe", lambda e: e.matmul(out, lhsT, rhs, start=start, stop=stop), r=r, w=w)

    def tp(out, in_, r, w, n=128):
        op("pe", lambda e: e.transpose(out, in_, cst[0:n, 0, 0:n]), r=list(r) + ["cst"], w=w)

    op("sp", lambda e: e.dma_start(out=cst[:], in_=cst_d), w=["cst"], dma=True)
    op("dve", lambda e: e.memset(ones_bf[:], 1.0), w=["ones_bf"])
    op("dve", lambda e: e.memset(ones_f[:], 1.0), w=["ones_f"])
    op("dve", lambda e: e.memset(epsc[:, 0:1], LN_EPS), w=["epsc"])
    op("dve", lambda e: e.memset(epsc[:, 1:2], RMS_EPS), w=["epsc"])
    op("dve", lambda e: e.memset(Sst[:], 0.0), w=["Sst"])

    def load_cols(src_ap, nrows, dst):
        op("sp", lambda e: e.dma_start(out=colraw[0:nrows, :], in_=src_ap), w=["colraw"], dma=True)
        b = bank("b")
        tp(psum[b][:, 0:nrows], colraw[0:nrows, :], r=["colraw"], w=[PK(b)], n=nrows)
        op("dve", lambda e: e.tensor_copy(out=dst, in_=psum[b][:, 0:nrows]), r=[PK(b)], w=["cols"])

    load_cols(lng_d.rearrange("r (c p) -> (r c) p", p=128), 96, gcol[:])
    load_cols(lnb_d.rearrange("r (c p) -> (r c) p", p=128), 96, bcol[:])
    op("sp", lambda e: e.dma_start(out=colraw[0:16, :], in_=hglb_d.rearrange("l (h p) -> (l h) p", p=128)), w=["colraw"], dma=True, semkey="colraw_a")
    op("sp", lambda e: e.dma_start(out=colraw[16:32, :], in_=hgn_d.rearrange("l (h p) -> (l h) p", p=128)), r=["colraw"], w=["colraw2"], dma=True, semkey="colraw_b")
    _b = bank("b")
    tp(psum[_b][:, 0:32], colraw[0:32, :], r=["colraw", "colraw2"], w=[PK(_b)], n=32)
    op("dve", lambda e: e.tensor_copy(out=hcol[:], in_=psum[_b][:, 0:32]), r=[PK(_b)], w=["cols"])
    ts(agcol[:], gcol[:], ALPHA, None, ALU.mult, None, r=["cols"], w=["acols"])
    ts(abcol[:], bcol[:], ALPHA, None, ALU.mult, None, r=["cols"], w=["acols"])

    def lower_bounds(src, dst_lb, dst_oml, dst_noml, e_ap, s_ap, r, w):
        act(e_ap, src, AF.Exp, r=r, w=w)
        tt(s_ap, e_ap[:, 0], e_ap[:, 1], ALU.add, r=w, w=w)
        tt(s_ap, s_ap, e_ap[:, 2], ALU.add, r=w, w=w)
        tt(s_ap, s_ap, e_ap[:, 3], ALU.add, r=w, w=w)
        op("dve", lambda e: e.reciprocal(out=s_ap, in_=s_ap), r=w, w=w)
        for l in range(1, 4):
            tt(e_ap[:, l], e_ap[:, l], s_ap, ALU.mult, r=w, w=w)
        op("dve", lambda e: e.memset(e_ap[:, 0], 0.0), r=w, w=w)
        tt(e_ap[:, 2], e_ap[:, 2], e_ap[:, 1], ALU.add, r=w, w=w)
        tt(e_ap[:, 3], e_ap[:, 3], e_ap[:, 2], ALU.add, r=w, w=w)

    lower_bounds(hcol[:, 0:16].rearrange("p (l h) -> p l h", l=4), None, None, None,
                 lbc[:, 1], lbc[:, 0, 0], r=["cols"], w=["lbc"])
    ts(lbc[:, 2], lbc[:, 1], -1.0, 1.0, ALU.mult, ALU.add, r=["lbc"], w=["lbc"])
    ts(lbc[:, 3], lbc[:, 1], 1.0, -1.0, ALU.mult, ALU.add, r=["lbc"], w=["lbc"])

    def resid_chunk(c, tb, b, stats_banks, first, last):
        sl = slice(tb * TB, (tb + 1) * TB)
        xk = ("xa", c, tb)
        stt(xa[:, c, sl], psum[b][:], 0.5, xa[:, c, sl], ALU.mult, ALU.add, r=[PK(b), xk], w=[xk])
        i0 = tbf_next()
        i1 = tbf_next()
        act(tbf[:, i0, :], xa[:, c, sl], AF.Identity, r=[xk], w=[("tbf", i0)])
        act(tbf[:, i1, :], xa[:, c, sl], AF.Square, r=[xk], w=[("tbf", i1)])
        s1, s2 = stats_banks
        mm(psum[s1][:], ones_bf[:], tbf[:, i0, :], first, last, r=["ones_bf", ("tbf", i0)], w=[PK(s1)])
        mm(psum[s2][:], ones_bf[:], tbf[:, i1, :], first, last, r=["ones_bf", ("tbf", i1)], w=[PK(s2)])

    def ln_finalize(tb, stats_banks, col0, final):
        sl = slice(tb * TB, (tb + 1) * TB)
        s1, s2 = stats_banks
        mean, var, rstd = lnst[:, 0, :], lnst[:, 1, :], lnst[:, 2, :]
        km, kv, kr = ("lnst", 0), ("lnst", 1), ("lnst", 2)
        act(mean, psum[s1][:], AF.Identity, r=[PK(s1)], w=[km], scale=1.0 / D)
        tt(var, mean, mean, ALU.mult, r=[km], w=[kv])
        stt(var, psum[s2][:], 1.0 / D, var, ALU.mult, ALU.subtract, r=[PK(s2), kv], w=[kv])
        act(var, var, AF.Ln, r=[kv, "epsc"], w=[kv], bias=epsc[:, 0:1], scale=1.0)
        act(rstd, var, AF.Exp, r=[kv], w=[kr], scale=-0.5)
        stt(mean, mean, -1.0, rstd, ALU.mult, ALU.mult, r=[km, kr], w=[km])
        for c in range(KC):
            xk = ("xa", c, tb)
            it_ = tmp_next()
            t = tmp[:, it_, :]
            kt = ("tmp", it_)
            tt(t, xa[:, c, sl], rstd, ALU.mult, r=[xk, kr], w=[kt])
            tt(t, t, mean, ALU.add, r=[kt, km], w=[kt])
            ci = col0 + c
            if final:
                act(xa[:, c, sl], t, AF.Identity, r=[kt, "cols"], w=[xk], scale=gcol[:, ci:ci + 1], bias=bcol[:, ci:ci + 1])
            else:
                act(xa[:, c, sl], t, AF.Identity, r=[kt, "acols"], w=[xk], scale=agcol[:, ci:ci + 1], bias=abcol[:, ci:ci + 1])
                act(xb[:, c, sl], t, AF.Identity, r=[kt, "cols"], w=[("xb", c, tb)], scale=gcol[:, ci:ci + 1], bias=bcol[:, ci:ci + 1])

    XB = lambda tb: [("xb", c, tb) for c in range(KC)]

    def ffn(l, which, lnidx, final):
        w13 = w13_d[which][l].rearrange("(kc p) n -> p kc n", p=128)
        w2 = w2_d[which][l].rearrange("(j p) n -> p j n", p=128)
        for g in range(11):
            ia, ib = wk_next(), wk_next()
            load_wk(ia, w13[:, :, g * 256:(g + 1) * 256])
            load_wk(ib, w13[:, :, DFF + g * 256: DFF + (g + 1) * 256])
            for jj in range(2):
                j = 2 * g + jj
                for tb in range(2):
                    sl = slice(tb * TB, (tb + 1) * TB)
                    pa, pb = bank("a"), bank("a")
                    for kc in range(KC):
                        mm(psum[pa][:], wk[ia][:, kc, jj * 128:(jj + 1) * 128], xb[:, kc, sl], kc == 0, kc == KC - 1,
                           r=[("wk", ia), ("xb", kc, tb)], w=[PK(pa)])
                    for kc in range(KC):
                        mm(psum[pb][:], wk[ib][:, kc, jj * 128:(jj + 1) * 128], xb[:, kc, sl], kc == 0, kc == KC - 1,
                           r=[("wk", ib), ("xb", kc, tb)], w=[PK(pb)])
                    it_ = tmp_next()
                    act(tmp[:, it_, :], psum[pa][:], AF.Silu, r=[PK(pa)], w=[("tmp", it_)])
                    hap, hk = hT(j, tb)
                    tt(hap, tmp[:, it_, :], psum[pb][:], ALU.mult, r=[("tmp", it_), PK(pb)], w=hk)
        stats = {0: (6, 7), 1: (2, 3)}
        for c in range(KC):
            ib = c % 2
            op("pool", lambda e, ib=ib, c=c: e.dma_start(out=w2b[ib][:], in_=w2[:, :, c * 128:(c + 1) * 128]),
               w=[("w2b", ib)], dma=True)
            for tb in range(2):
                sl = slice(tb * TB, (tb + 1) * TB)
                b = bank("b")
                for j in range(NJ):
                    hap, hk = hT(j, tb)
                    mm(psum[b][:], w2b[ib][:, j, :], hap, j == 0, j == NJ - 1, r=[("w2b", ib)] + hk, w=[PK(b)])
                resid_chunk(c, tb, b, stats[tb], c == 0, c == KC - 1)
        for tb in range(2):
            ln_finalize(tb, stats[tb], lnidx * 8, final)

    def gelu_from_psum(b, out_ap, out_keys):
        ih, iw = tmp_next(), tmp_next()
        xh, wv = tmp[:, ih, :], tmp[:, iw, :]
        kh_, kw_ = ("tmp", ih), ("tmp", iw)
        act(xh, psum[b][:], AF.Identity, r=[PK(b)], w=[kh_], scale=0.5)
        act(wv, psum[b][:], AF.Square, r=[PK(b)], w=[kw_])
        ts(wv, wv, GC2, 1.0, ALU.mult, ALU.add, r=[kw_], w=[kw_])
        tt(wv, wv, xh, ALU.mult, r=[kw_, kh_], w=[kw_])
        act(wv, wv, AF.Tanh, r=[kw_], w=[kw_], scale=2.0 * GC1)
        stt(out_ap, wv, 1.0, xh, ALU.add, ALU.mult, r=[kw_, kh_], w=out_keys)

    def mixer_setup(l):
        raw, rk = scr_f32(0, 8)
        raw3 = raw.rearrange("p (a n) -> p a n", a=4)
        op("sp", lambda e: e.dma_start(out=raw, in_=hglb_d.rearrange("(o l) n -> o (l n)", o=1).partition_broadcast(128)), w=rk, dma=True, semkey="rowld")
        lower_bounds(raw3, None, None, None, raw3, tmp[:, 0, :], r=rk + [("tmp", 0)], w=rk + [("tmp", 0)])
        op("dve", lambda e: e.tensor_copy(out=rows[:, 0, :], in_=raw3[:, l]), r=rk, w=["rows01"])
        ts(rows[:, 1, :], raw3[:, l], -1.0, 1.0, ALU.mult, ALU.add, r=rk, w=["rows01"])
        op("sp", lambda e: e.dma_start(out=rows[:, 2, :], in_=sglg_d[l:l + 1, :].partition_broadcast(128)), w=["rows2"], dma=True)
        op("sp", lambda e: e.dma_start(out=rows[:, 3, :], in_=sglb_d[l:l + 1, :].partition_broadcast(128)), w=["rows3"], dma=True)
        op("sp", lambda e: e.dma_start(out=bsrow[:], in_=sgbs_d[l:l + 1, :]), w=["bsrow"], dma=True)
        i0 = tmp_next()
        wraw = tmp[:, i0, :].rearrange("p (g s) -> p g s", g=4)
        op("sp", lambda e: e.dma_start(out=wraw, in_=sgws_d[l].rearrange("g t s -> t g s")), w=[("tmp", i0)], dma=True, semkey="wsld")
        b = bank("b")
        for g in range(4):
            tp(psum[b][:, g * 128:(g + 1) * 128], wraw[:, g, :], r=[("tmp", i0)], w=[PK(b)])
        tt(wsT[:], psum[b][:].rearrange("p (g t) -> p g t", g=4), cst[:, 6:10, :], ALU.mult, r=[PK(b), "cst"], w=["wsT"])

    def proj_fm(iw, col, tb, b):
        sl = slice(tb * TB, (tb + 1) * TB)
        for kc in range(KC):
            mm(psum[b][:], wk[iw][:, kc, col:col + 128], xb[:, kc, sl], kc == 0, kc == KC - 1,
               r=[("wk", iw), ("xb", kc, tb)], w=[PK(b)])

    def proj_tm(iw0, iw1, tb, t4, b):
        c0 = tb * TB + t4 * 128
        for half, iw in enumerate((iw0, iw1)):
            for kc in range(KC):
                mm(psum[b][:, half * 256:(half + 1) * 256], xb[:, kc, c0:c0 + 128], wk[iw][:, kc, :], kc == 0, kc == KC - 1,
                   r=[("wk", iw), ("xb", kc, tb)], w=[PK(b)])

    def mixer(l, hf):
        win = win_d[l].rearrange("(kc p) n -> p kc n", p=128)
        wa = wa_d[l].rearrange("(kc p) n -> p kc n", p=128)
        wbb = wbb_d[l].rearrange("(kc p) n -> p kc n", p=128)
        wo = wo_d[l].rearrange("(kc p) n -> p kc n", p=128)
        mixer_setup(l)

        def load_in(col0):
            i0, i1 = wk_next(), wk_next()
            load_wk(i0, win[:, :, col0:col0 + 256])
            load_wk(i1, win[:, :, col0 + 256:col0 + 512])
            return i0, i1

        for tb in range(2):
            sl = slice(tb * TB, (tb + 1) * TB)
            wg = load_in(1536)
            for h in range(4):
                b = bank("a")
                proj_fm(wg[h // 2], (h % 2) * 128, tb, b)
                it_ = tmp_next()
                act(tmp[:, it_, :], psum[b][:], AF.Silu, r=[PK(b)], w=[("tmp", it_)])
                ts(sgT[:, h, :], tmp[:, it_, :], hcol[:, 16 + l * 4 + h:16 + l * 4 + h + 1], None, ALU.mult, None,
                   r=[("tmp", it_), "cols"], w=sgT_k)
            wu = load_in(2048)
            for g in range(4):
                b = bank("a")
                proj_fm(wu[g // 2], (g % 2) * 128, tb, b)
                gelu_from_psum(b, guT[:, g, :], guT_k)
            wv_ = load_in(2560)
            for t4 in range(4):
                b = bank("a")
                proj_tm(wv_[0], wv_[1], tb, t4, b)
                gelu_from_psum(b, gv[:, t4, :], gv_k)
                op("dve", lambda e, t4=t4: e.bn_stats(out=small[:, t4 * 6:(t4 + 1) * 6], in_=gv[:, t4, :]), r=gv_k, w=["small_bn"])
                op("dve", lambda e, t4=t4: e.bn_aggr(out=small[:, 24 + t4 * 2:24 + t4 * 2 + 2], in_=small[:, t4 * 6:(t4 + 1) * 6]),
                   r=["small_bn"], w=["small_mv"])
            mvv = small[:, 24:32].rearrange("p (t two) -> p t two", two=2)
            act(small[:, 32:36], mvv[:, :, 1], AF.Ln, r=["small_mv", "epsc"], w=["small_rs"], bias=epsc[:, 0:1], scale=1.0)
            act(small[:, 32:36], small[:, 32:36], AF.Exp, r=["small_rs"], w=["small_rs"], scale=-0.5)
            stt(small[:, 36:40], mvv[:, :, 0], -1.0, small[:, 32:36], ALU.mult, ALU.mult, r=["small_mv", "small_rs"], w=["small_rs"])
            for t4 in range(4):
                it_ = tmp_next()
                t = tmp[:, it_, :]
                act(t, gv[:, t4, :], AF.Identity, r=gv_k + ["small_rs"], w=[("tmp", it_)],
                    scale=small[:, 32 + t4:33 + t4], bias=small[:, 36 + t4:37 + t4])
                tt(t, t, rows[:, 2, :], ALU.mult, r=[("tmp", it_), "rows2"], w=[("tmp", it_)])
                tt(vn[:, t4, :], t, rows[:, 3, :], ALU.add, r=[("tmp", it_), "rows3"], w=vn_k)
            wf = load_in(512)
            for t4 in range(4):
                b = bank("a")
                proj_tm(wf[0], wf[1], tb, t4, b)
                ie, ik = tmp_next(), tmp_next()
                e_, k_ = tmp[:, ie, :], tmp[:, ik, :]
                ke, kk = ("tmp", ie), ("tmp", ik)
                act(e_, psum[b][:], AF.Exp, r=[PK(b)], w=[ke], scale=-1.0)
                ts(e_, e_, 1.0, None, ALU.add, None, r=[ke], w=[ke])
                op("dve", lambda e, e_=e_: e.reciprocal(out=e_, in_=e_), r=[ke], w=[ke])
                tt(e_, e_, rows[:, 1, :], ALU.mult, r=[ke, "rows01"], w=[ke])
                tt(e_, e_, rows[:, 0, :], ALU.add, r=[ke, "rows01"], w=[ke])
                ts(k_, e_, -1.0, 1.0, ALU.mult, ALU.add, r=[ke], w=[kk])
                act(lf[:, t4, :], e_, AF.Ln, r=[ke], w=lf_k)
                b2 = bank("b")
                mm(psum[b2][:], cst[:, 1, :], lf[:, t4, :], True, True, r=["cst"] + lf_k, w=[PK(b2)])
                act(e_, psum[b2][:], AF.Exp, r=[PK(b2)], w=[ke])
                tt(kh[:, t4, :], k_, e_, ALU.mult, r=[kk, ke], w=kh_k)
            wi = load_in(1024)
            for t4 in range(4):
                b = bank("a")
                proj_tm(wi[0], wi[1], tb, t4, b)
                act(it[:, t4, :], psum[b][:], AF.Identity, r=[PK(b)], w=it_k)
            wq = load_in(0)
            for h in range(4):
                b = bank("a")
                proj_fm(wf[h // 2], (h % 2) * 128, tb, b)
                ie, ibt, ieq, iek = tmp_next(), tmp_next(), tmp_next(), tmp_next()
                e_, bTs, eq, ek = tmp[:, ie, :], tmp[:, ibt, :], tmp[:, ieq, :], tmp[:, iek, :]
                ke, kbt, keq, kek = ("tmp", ie), ("tmp", ibt), ("tmp", ieq), ("tmp", iek)
                act(e_, psum[b][:], AF.Exp, r=[PK(b)], w=[ke], scale=-1.0)
                ts(e_, e_, 1.0, None, ALU.add, None, r=[ke], w=[ke])
                op("dve", lambda e, e_=e_: e.reciprocal(out=e_, in_=e_), r=[ke], w=[ke])
                ts(e_, e_, lbc[:, 3, l, h:h + 1], lbc[:, 2, l, h:h + 1], ALU.mult, ALU.add, r=[ke, "lbc"], w=[ke])
                b2 = bank("b")
                for t4 in range(4):
                    mm(psum[b2][:, t4 * 128:(t4 + 1) * 128], lf[:, t4, h * 128:(h + 1) * 128], cst[:, 2, :], True, True,
                       r=lf_k + ["cst"], w=[PK(b2)])
                act(bTs, psum[b2][:], AF.Identity, r=[PK(b2)], w=[kbt])
                dk = ("dec", h)
                bT3 = bTs.rearrange("p (c n) -> p c n", n=64)
                ts(dec[:, h, 0, :], bT3[:, :, 31], -1.0, None, ALU.mult, None, r=[kbt], w=[dk])
                act(dec[:, h, 1, :], bT3[:, :, 31], AF.Exp, r=[kbt], w=[dk])
                act(dec[:, h, 2, :], bT3[:, :, 63], AF.Exp, r=[kbt], w=[dk])
                for ck in range(8):
                    cs = slice(ck * 64, (ck + 1) * 64)
                    act(eq[:, cs], bTs[:, cs], AF.Exp, r=[kbt, dk], w=[keq], bias=dec[:, h, 0, ck:ck + 1], scale=1.0)
                    act(ek[:, cs], bTs[:, cs], AF.Exp, r=[kbt], w=[kek], bias=bTs[:, ck * 64 + 31:ck * 64 + 32], scale=-1.0)
                tt(ktT[:, h, :], e_, ek, ALU.mult, r=[ke, kek], w=[("ktT", h)])
                b3 = bank("a")
                proj_fm(wq[h // 2], (h % 2) * 128, tb, b3)
                tt(qtT[:, h, :], psum[b3][:], eq, ALU.mult, r=[PK(b3), keq], w=[("qtT", h)])
            for t4 in range(4):
                ts_ = slice(t4 * 128, (t4 + 1) * 128)
                bp = bank("b")
                for h in range(4):
                    mm(psum[bp][:, h * 128:(h + 1) * 128], ktT[:, h, ts_], qtT[:, h, ts_], True, True,
                       r=[("ktT", h), ("qtT", h)], w=[PK(bp)])
                ip = t4 % 2
                tt(PTs[:, ip, :].rearrange("p (h t) -> p h t", h=4), psum[bp][:].rearrange("p (h t) -> p h t", h=4), cst[:, 2:6, :],
                   ALU.mult, r=[PK(bp), "cst"], w=[("PTs", ip)])
                bo = 6
                for h in range(4):
                    dk = ("dec", h)
                    Sh = Sst[:, l, h, :]
                    sk = ("S", l, h)
                    ck0 = 2 * t4
                    hs = slice(h * 128, (h + 1) * 128)
                    ts(Smb[:, 0, h, :], Sh, dec[:, h, 1, ck0:ck0 + 1], None, ALU.mult, None, r=[sk, dk], w=[("Smb", 0, h)])
                    mm(psum[bo][:, hs], it[:, t4, hs], PTs[:, ip, hs], True, False, r=it_k + [("PTs", ip)], w=[PK(bo)], sgc=True)
                    mm(psum[bo][:, h * 128:h * 128 + 64], Smb[:, 0, h, :], qtT[:, h, t4 * 128:t4 * 128 + 64], False, False,
                       r=[("Smb", 0, h), ("qtT", h)], w=[PK(bo)], sgc=True)
                    bs_ = bank("a")
                    mm(psum[bs_][:, 0:128], kh[0:64, t4, hs], it[0:64, t4, hs], True, True, r=kh_k + it_k, w=[PK(bs_)])
                    stt(Sh, Sh, dec[:, h, 2, ck0:ck0 + 1], psum[bs_][:, 0:128], ALU.mult, ALU.add, r=[sk, dk, PK(bs_)], w=[sk])
                    ts(Smb[:, 1, h, :], Sh, dec[:, h, 1, ck0 + 1:ck0 + 2], None, ALU.mult, None, r=[sk, dk], w=[("Smb", 1, h)])
                    mm(psum[bo][:, h * 128 + 64:h * 128 + 128], Smb[:, 1, h, :], qtT[:, h, t4 * 128 + 64:t4 * 128 + 128], False, True,
                       r=[("Smb", 1, h), ("qtT", h)], w=[PK(bo)], sgc=True)
                    bs2 = bank("a")
                    mm(psum[bs2][:, 0:128], kh[64:128, t4, hs], it[64:128, t4, hs], True, True, r=kh_k + it_k, w=[PK(bs2)])
                    stt(Sh, Sh, dec[:, h, 2, ck0 + 1:ck0 + 2], psum[bs2][:, 0:128], ALU.mult, ALU.add, r=[sk, dk, PK(bs2)], w=[sk])
                iq = tbf_next()
                act(tbf[:, iq, :], psum[bo][:], AF.Square, r=[PK(bo)], w=[("tbf", iq)])
                mm(psum[7][:], ones_bf[:], tbf[:, iq, :], True, True, r=["ones_bf", ("tbf", iq)], w=[PK(7)])
                ir = tmp_next()
                rs = tmp[:, ir, :]
                act(rs, psum[7][:], AF.Ln, r=[PK(7), "epsc"], w=[("tmp", ir)], bias=epsc[:, 1:2], scale=1.0 / 128)
                act(rs, rs, AF.Exp, r=[("tmp", ir)], w=[("tmp", ir)], scale=-0.5)
                tt(rs, psum[bo][:], rs, ALU.mult, r=[PK(bo), ("tmp", ir)], w=[("tmp", ir)])
                tt(ogT[:, :, ts_], rs.rearrange("p (h t) -> p h t", h=4), sgT[:, :, ts_], ALU.mult, r=[("tmp", ir)] + sgT_k, w=og_k)
            for g in range(4):
                b = bank("b")
                for t4 in range(4):
                    ts_ = slice(t4 * 128, (t4 + 1) * 128)
                    mm(psum[b][:, ts_], vn[:, t4, g * 128:(g + 1) * 128], wsT[:, g, :], True, False, r=vn_k + ["wsT"], w=[PK(b)])
                    mm(psum[b][:, ts_], ones_f[0:1, :], bsrow[0:1, g * 128:(g + 1) * 128], False, True, r=["ones_f", "bsrow"], w=[PK(b)])
                tt(hbT[:, g, :], psum[b][:], guT[:, g, :], ALU.mult, r=[PK(b)] + guT_k, w=hb_k)
            v5 = lambda t: t[:].rearrange("p k n -> p (k n)").rearrange("p (a k n) -> p a k n", a=2, k=4)
            for c2 in range(4):
                iab = wk_next()
                load_wk(iab, wa[:, :, c2 * 256:(c2 + 1) * 256], view=lambda t: v5(t)[:, 0])
                load_wk(iab, wbb[:, :, c2 * 256:(c2 + 1) * 256], view=lambda t: v5(t)[:, 1])
                iga, igb = wk_next(), wk_next()
                load_wk(iga, win[:, :, 3072 + c2 * 256:3072 + (c2 + 1) * 256])
                load_wk(igb, win[:, :, 4096 + c2 * 256:4096 + (c2 + 1) * 256])
                for cc in range(2):
                    c = 2 * c2 + cc
                    pya, pza, pyb, pzb = 0, 1, 2, 3
                    wav = v5(wk[iab])[:, 0]
                    wbv = v5(wk[iab])[:, 1]
                    co = cc * 128
                    for k4 in range(4):
                        mm(psum[pya][:], wav[:, k4, co:co + 128], ogT[:, k4, :], k4 == 0, k4 == 3, r=[("wk", iab)] + og_k, w=[PK(pya)])
                    proj_fm(iga, cc * 128, tb, pza)
                    for k4 in range(4):
                        mm(psum[pyb][:], wbv[:, k4, co:co + 128], hbT[:, k4, :], k4 == 0, k4 == 3, r=[("wk", iab)] + hb_k, w=[PK(pyb)])
                    proj_fm(igb, cc * 128, tb, pzb)
                    i1, i2 = tmp_next(), tmp_next()
                    t1, t2 = tmp[:, i1, :], tmp[:, i2, :]
                    act(t1, psum[pza][:], AF.Tanh, r=[PK(pza)], w=[("tmp", i1)], scale=0.5)
                    act(t2, psum[pzb][:], AF.Tanh, r=[PK(pzb)], w=[("tmp", i2)], scale=0.5)
                    stt(t1, t1, 1.0, psum[pya][:], ALU.add, ALU.mult, r=[("tmp", i1), PK(pya)], w=[("tmp", i1)])
                    stt(t2, t2, 1.0, psum[pyb][:], ALU.add, ALU.mult, r=[("tmp", i2), PK(pyb)], w=[("tmp", i2)])
                    tt(yT[:, c, :], t1, t2, ALU.add, r=[("tmp", i1), ("tmp", i2)], w=[yT_k[c]])
            for c2 in range(4):
                iw = wk_next()
                load_wk(iw, wo[:, :, c2 * 256:(c2 + 1) * 256])
                for cc in range(2):
                    c = 2 * c2 + cc
                    b = bank("b")
                    for kc in range(KC):
                        mm(psum[b][:], wk[iw][:, kc, cc * 128:(cc + 1) * 128], yT[:, kc, :], kc == 0, kc == KC - 1,
                           r=[("wk", iw), yT_k[kc]], w=[PK(b)])
                    resid_chunk(c, tb, b, (6, 7), c == 0, c == KC - 1)
            ln_finalize(tb, (6, 7), (l * 3 + 1) * 8, False)

    dec = sb("dec", [128, 4, 3, 8], F32)

    def load_x(hf):
        xin = x_d[hf * NT:(hf + 1) * NT, :].rearrange("(t p) d -> p t d", p=128)
        for t8 in range(8):
            st_ap, st_k = scr_f32(t8 * 4, 4)
            op("sp", lambda e, st_ap=st_ap, t8=t8: e.dma_start(out=st_ap, in_=xin[:, t8, :]), w=st_k, dma=True, semkey=("xst", t8))
        for c in range(KC):
            for tb in range(2):
                b = bank("a")
                rk = []
                for i in range(4):
                    t8 = tb * 4 + i
                    st_ap, st_k = scr_f32(t8 * 4, 4)
                    tp(psum[b][:, i * 128:(i + 1) * 128], st_ap[:, c * 128:(c + 1) * 128], r=st_k, w=[PK(b)])
                sl = slice(tb * TB, (tb + 1) * TB)
                act(xa[:, c, sl], psum[b][:], AF.Identity, r=[PK(b)], w=[("xa", c, tb)], scale=ALPHA)
                op("dve", lambda e, c=c, sl=sl, b=b: e.tensor_copy(out=xb[:, c, sl], in_=psum[b][:]), r=[PK(b), ("xa", c, tb)], w=[("xb", c, tb)])

    def store_x(hf):
        xout = out_d[hf * NT:(hf + 1) * NT, :].rearrange("(t p) d -> p t d", p=128)
        for t8 in range(8):
            tb, i = t8 // 4, t8 % 4
            st_ap, st_k = scr_f32(t8 * 4, 4)
            for half in range(2):
                b = bank("a")
                for cc in range(4):
                    c = half * 4 + cc
                    tp(psum[b][:, cc * 128:(cc + 1) * 128], xa[:, c, t8 * 128:(t8 + 1) * 128], r=[("xa", c, tb)], w=[PK(b)])
                if half == 0:
                    act(st_ap[:, 0:512], psum[b][:], AF.Identity, r=[PK(b)], w=st_k)
                else:
                    op("dve", lambda e, st_ap=st_ap, b=b: e.tensor_copy(out=st_ap[:, 512:1024], in_=psum[b][:]), r=[PK(b)], w=st_k)
            op("sp", lambda e, st_ap=st_ap, t8=t8: e.dma_start(out=xout[:, t8, :], in_=st_ap), r=st_k, w=[("out", hf, t8)], dma=True,
               semkey=("ost", t8))

    def dbg_dump(i):
        if debug:
            op("sp", lambda e, i=i: e.dma_start(out=dbg_d[i], in_=xa[:]), r=[("xa", c, tb) for c in range(KC) for tb in range(2)],
               w=[("dbg", i)], dma=True, semkey=("dbg", i))

    for hf in range(nhalf):
        load_x(hf)
        for l in range(depth):
            last = (l == depth - 1)
            if STAGES >= 1:
                ffn(l, 0, l * 3 + 0, False)
            if hf == 0 and l == 0:
                dbg_dump(0)
            if STAGES >= 2:
                mixer(l, hf)
            if hf == 0 and l == 0:
                dbg_dump(1)
            if STAGES >= 3:
                ffn(l, 1, l * 3 + 2, last)
            if hf == 0 and l == 0:
                dbg_dump(2)
        store_x(hf)
    S.emit()
    es.close()
    return nc


_CACHE = {}


def kernel(x, ffn1_w13, ffn1_w2, ffn2_w13, ffn2_w2, ln_g, ln_b, w_in, hg_lb, hg_norm_g,
           sg_ln_g, sg_ln_b, sg_ws, sg_bs, w_branch_a, w_branch_b, w_out):
    f = lambda a: np.ascontiguousarray(np.asarray(a, dtype=np.float32))
    x = f(x)
    B = x.shape[0]
    if "nc" not in _CACHE:
        _CACHE["nc"] = build_program(DEPTH, 2, False)
    nc = _CACHE["nc"]
    shared = dict(
        ffn1_w13=f(ffn1_w13), ffn1_w2=f(ffn1_w2), ffn2_w13=f(ffn2_w13), ffn2_w2=f(ffn2_w2),
        ln_g=f(ln_g).reshape(DEPTH * 3, D), ln_b=f(ln_b).reshape(DEPTH * 3, D), w_in=f(w_in),
        hg_lb=f(hg_lb), hg_norm_g=f(hg_norm_g), sg_ln_g=f(sg_ln_g), sg_ln_b=f(sg_ln_b),
        sg_ws=f(sg_ws), sg_bs=f(sg_bs).reshape(DEPTH, 512), w_branch_a=f(w_branch_a),
        w_branch_b=f(w_branch_b), w_out=f(w_out), consts=make_consts())
    in_maps = [dict(shared, x=x[b]) for b in range(B)]
    res = run_bass_kernel_spmd(nc, in_maps, core_ids=list(range(B)))
    return np.stack([np.asarray(r["out"], dtype=np.float32) for r in res.results], axis=0)
```

```python
import contextlib
import numpy as np
import concourse.bass as bass
import concourse.mybir as mybir
from concourse.bass_utils import run_bass_kernel_spmd

F32 = mybir.dt.float32
BF16 = mybir.dt.bfloat16
AF = mybir.ActivationFunctionType
ALU = mybir.AluOpType

D = 1024
KC = 8
DFF = 2816
NJ = 22
DEPTH = 4
NT = 1024
TB = 512
DIN = 5120
ALPHA = (2 * DEPTH) ** 0.25
LN_EPS = 1e-5
RMS_EPS = 1e-6
GC1 = 0.7978845608028654
GC2 = 0.044715
COMPUTE = ("pe", "act", "dve", "pool")
STAGES = 3
EMBED_WAITS = True


class Sched:
    def __init__(self, nc):
        self.nc = nc
        self.ops = []
        self.lastw = {}
        self.rd_eng = {}
        self.rd_dma = {}

    def op(self, eng, fn, r=(), w=(), dma=False, semkey=None):
        idx = len(self.ops)
        deps = set()
        for k in r:
            if k in self.lastw:
                deps.add(self.lastw[k])
        for k in w:
            if k in self.lastw:
                deps.add(self.lastw[k])
            deps.update(self.rd_eng.get(k, {}).values())
            deps.update(self.rd_dma.get(k, ()))
        for k in r:
            if dma:
                self.rd_dma.setdefault(k, []).append(idx)
            else:
                self.rd_eng.setdefault(k, {})[eng] = idx
        for k in w:
            self.lastw[k] = idx
            self.rd_eng[k] = {}
            self.rd_dma[k] = []
        if dma and semkey is None:
            semkey = w[0]
        self.ops.append(dict(eng=eng, fn=fn, deps=deps, dma=dma, semkey=semkey))
        return idx

    def emit(self):
        nc = self.nc
        ops = self.ops
        for i, o in enumerate(ops):
            best = {}
            dd = []
            for d in o["deps"]:
                p = ops[d]
                if p["dma"]:
                    dd.append(d)
                else:
                    if p["eng"] == "pe" and o["eng"] == "pe" and not o["dma"]:
                        continue
                    if d > best.get(p["eng"], -1):
                        best[p["eng"]] = d
            o["cdeps"] = list(best.values()) + dd
        needs = set()
        for o in ops:
            needs.update(o["cdeps"])
        stack = contextlib.ExitStack()
        eng_sem = {e: stack.enter_context(nc.semaphore("s_" + e)) for e in COMPUTE}
        dma_sem = {}
        eng_cnt = {e: 0 for e in COMPUTE}
        dma_cnt = {}
        for i, o in enumerate(ops):
            if o["dma"]:
                k = o["semkey"]
                if k not in dma_sem:
                    dma_sem[k] = stack.enter_context(nc.semaphore("d%d" % len(dma_sem)))
                    dma_cnt[k] = 0
                dma_cnt[k] += 16
                o["sig"] = (dma_sem[k], dma_cnt[k])
            elif i in needs:
                eng_cnt[o["eng"]] += 1
                o["sig"] = (eng_sem[o["eng"]], eng_cnt[o["eng"]])
            else:
                o["sig"] = None
        streams = {}
        for i, o in enumerate(ops):
            streams.setdefault(o["eng"], []).append(i)

        def run(ename, e):
            waited = {}
            for i in streams.get(ename, []):
                o = ops[i]
                need = {}
                for d in o["cdeps"]:
                    sem, val = ops[d]["sig"]
                    key = id(sem)
                    if waited.get(key, 0) >= val:
                        continue
                    if key not in need or need[key][1] < val:
                        need[key] = (sem, val)
                need = list(need.values())
                for sem, val in need[:-1]:
                    e.wait_ge(sem, val)
                    waited[id(sem)] = val
                ins = o["fn"](e)
                if need:
                    sem, val = need[-1]
                    if EMBED_WAITS:
                        ins._wait_ge(sem, val)
                    else:
                        raise RuntimeError("unreachable")
                    waited[id(sem)] = val
                if o["sig"] is not None:
                    ins.then_inc(o["sig"][0], 16 if o["dma"] else 1)
            last = {}
            for i in streams.get(ename, []):
                o = ops[i]
                if o["dma"]:
                    last[id(o["sig"][0])] = o["sig"]
            for sem, val in last.values():
                if waited.get(id(sem), 0) < val:
                    e.wait_ge(sem, val)

        with nc.Block() as block:
            @block.tensor
            def _(e):
                run("pe", e)

            @block.scalar
            def _(e):
                run("act", e)

            @block.vector
            def _(e):
                run("dve", e)

            @block.gpsimd
            def _(e):
                run("pool", e)

            @block.sync
            def _(e):
                run("sp", e)
        stack.close()


def make_consts():
    s = np.arange(128)[:, None]
    t = np.arange(128)[None, :]
    same = (s // 64) == (t // 64)
    ident = np.eye(128, dtype=np.float32)
    ublk = ((s <= t) & same).astype(np.float32)
    mgt = ((s > t) & same).astype(np.float32)
    triu = (s <= t).astype(np.float32)
    c = np.zeros((128, 10, 128), np.float32)
    c[:, 0] = ident
    c[:, 1] = mgt
    for i in range(4):
        c[:, 2 + i] = ublk
        c[:, 6 + i] = triu
    return c


def build_program(depth=DEPTH, nhalf=2, debug=False):
    nc = bass.Bass("TRN2", target_bir_lowering=False)
    T = nhalf * NT
    dr = lambda name, shape, kind="ExternalInput": nc.dram_tensor(name, list(shape), F32, kind=kind).ap()
    x_d = dr("x", [T, D])
    w13_d = [dr("ffn1_w13", [DEPTH, D, 2 * DFF]), dr("ffn2_w13", [DEPTH, D, 2 * DFF])]
    w2_d = [dr("ffn1_w2", [DEPTH, DFF, D]), dr("ffn2_w2", [DEPTH, DFF, D])]
    lng_d = dr("ln_g", [DEPTH * 3, D])
    lnb_d = dr("ln_b", [DEPTH * 3, D])
    win_d = dr("w_in", [DEPTH, D, DIN])
    hglb_d = dr("hg_lb", [DEPTH, 512])
    hgn_d = dr("hg_norm_g", [DEPTH, 512])
    sglg_d = dr("sg_ln_g", [DEPTH, 512])
    sglb_d = dr("sg_ln_b", [DEPTH, 512])
    sgws_d = dr("sg_ws", [DEPTH, 4, 128, 128])
    sgbs_d = dr("sg_bs", [DEPTH, 512])
    wa_d = dr("w_branch_a", [DEPTH, 512, D])
    wbb_d = dr("w_branch_b", [DEPTH, 512, D])
    wo_d = dr("w_out", [DEPTH, D, D])
    cst_d = dr("consts", [128, 10, 128])
    out_d = dr("out", [T, D], kind="ExternalOutput")
    dbg_d = dr("dbg", [3, 128, KC, NT], kind="ExternalOutput") if debug else None
    lbrow_d = nc.dram_tensor("lbrow_scratch", [DEPTH, 512], F32).ap()

    es = contextlib.ExitStack()
    sb = lambda name, shape, dt: es.enter_context(nc.sbuf_tensor(name, list(shape), dt))
    S = Sched(nc)
    op = S.op

    xa = sb("xa", [128, KC, NT], F32)
    xb = sb("xb", [128, KC, NT], BF16)
    scr = sb("scr", [128, NJ * NT], BF16)
    wk = [sb("wk%d" % i, [128, KC, 256], BF16) for i in range(8)]
    w2b = [sb("w2b%d" % i, [128, NJ, 128], BF16) for i in range(2)]
    tmp = sb("tmp", [128, 6, 512], F32)
    tbf = sb("tbf", [128, 4, 512], BF16)
    cst = sb("cst", [128, 10, 128], F32)
    ones_bf = sb("ones_bf", [128, 128], BF16)
    ones_f = sb("ones_f", [1, 128], F32)
    epsc = sb("epsc", [128, 2], F32)
    colraw = sb("colraw", [128, 128], F32)
    gcol = sb("gcol", [128, 96], F32)
    bcol = sb("bcol", [128, 96], F32)
    agcol = sb("agcol", [128, 96], F32)
    abcol = sb("abcol", [128, 96], F32)
    hcol = sb("hcol", [128, 32], F32)
    lbc = sb("lbc", [128, 4, 4, 4], F32)
    Sst = sb("Sst", [128, depth, 4, 128], F32)
    Smb = sb("Smb", [128, 2, 4, 128], BF16)
    rows = sb("rows", [128, 4, 512], F32)
    bsrow = sb("bsrow", [1, 512], F32)
    wsT = sb("wsT", [128, 4, 128], BF16)
    qtT = sb("qtT", [128, 4, 512], BF16)
    ktT = sb("ktT", [128, 4, 512], BF16)
    small = sb("small", [128, 64], F32)
    lnst = sb("lnst", [128, 2, 2, 512], F32)
    PTs = sb("PTs", [128, 2, 512], BF16)
    psum = [es.enter_context(nc.psum_tensor("ps%d" % i, [128, 512], F32)) for i in range(8)]
    PK = lambda b: ("ps", b)

    def scr_bf(off_blk, nblk):
        return scr[:, off_blk * 512:(off_blk + nblk) * 512], [("scr", b) for b in range(off_blk, off_blk + nblk)]

    def scr_f32(off_blk, nblk):
        ap, keys = scr_bf(off_blk, nblk)
        return ap.bitcast(F32), keys

    def hT(j, tb):
        return scr[:, j * NT + tb * TB: j * NT + (tb + 1) * TB], [("scr", j * 2 + tb)]

    sgT_ap, sgT_k = scr_f32(0, 8)
    guT_ap, guT_k = scr_f32(8, 8)
    gv_ap, gv_k = scr_f32(16, 8)
    lf_ap, lf_k = scr_f32(24, 8)
    vn_ap, vn_k = scr_bf(32, 4)
    kh_ap, kh_k = scr_bf(36, 4)
    it_ap, it_k = scr_bf(40, 4)
    v3 = lambda ap: ap.rearrange("p (a n) -> p a n", a=4)
    sgT, guT, gv, lf, vn, kh, it = map(v3, (sgT_ap, guT_ap, gv_ap, lf_ap, vn_ap, kh_ap, it_ap))
    yT_ap, yT_k = scr_bf(16, 8)
    hb_ap, hb_k = scr_bf(24, 4)
    og_ap, og_k = scr_bf(28, 4)
    yT = yT_ap.rearrange("p (a n) -> p a n", a=8)
    hbT = v3(hb_ap)
    ogT = v3(og_ap)

    bankrot = {"a": [0, [0, 1, 2, 3]], "b": [0, [4, 5]]}

    def bank(g):
        st = bankrot[g]
        b = st[1][st[0] % len(st[1])]
        st[0] += 1
        return b

    wkrot = [0]

    def wk_next():
        i = wkrot[0] % 8
        wkrot[0] += 1
        return i

    def load_wk(i, src, view=None):
        dst = wk[i][:] if view is None else view(wk[i])
        op("pool", lambda e: e.dma_start(out=dst, in_=src), w=[("wk", i)], dma=True)

    tmprot = [0]

    def tmp_next():
        i = tmprot[0] % 6
        tmprot[0] += 1
        return i

    tbfrot = [0]

    def tbf_next():
        i = tbfrot[0] % 4
        tbfrot[0] += 1
        return i

    def act(out, in_, func, r, w, **kw):
        op("act", lambda e: e.activation(out=out, in_=in_, func=func, **kw), r=r, w=w)

    def tt(out, in0, in1, alu, r, w, eng="dve"):
        op(eng, lambda e: e.tensor_tensor(out=out, in0=in0, in1=in1, op=alu), r=r, w=w)

    def ts(out, in0, s1, s2, op0, op1, r, w):
        if s2 is None:
            op("dve", lambda e: e.tensor_scalar(out=out, in0=in0, scalar1=s1, scalar2=None, op0=op0), r=r, w=w)
        else:
            op("dve", lambda e: e.tensor_scalar(out=out, in0=in0, scalar1=s1, scalar2=s2, op0=op0, op1=op1), r=r, w=w)

    def stt(out, in0, scalar, in1, op0, op1, r, w):
        op("dve", lambda e: e.scalar_tensor_tensor(out=out, in0=in0, scalar=scalar, in1=in1, op0=op0, op1=op1), r=r, w=w)

    def mm(out, lhsT, rhs, start, stop, r, w, sgc=False):
        if sgc:
            op("pe", lambda e: e.matmul(out, lhsT, rhs, start=start, stop=stop, skip_group_check=True), r=r, w=w)
        else:
            op("pe", lambda e: e.matmul(out, lhsT, rhs, start=start, stop=stop), r=r, w=w)

    def tp(out, in_, r, w, n=128):
        op("pe", lambda e: e.transpose(out, in_, cst[0:n, 0, 0:n]), r=list(r) + ["cst"], w=w)

    op("sp", lambda e: e.dma_start(out=cst[:], in_=cst_d), w=["cst"], dma=True)
    op("dve", lambda e: e.memset(ones_bf[:], 1.0), w=["ones_bf"])
    op("dve", lambda e: e.memset(ones_f[:], 1.0), w=["ones_f"])
    op("dve", lambda e: e.memset(epsc[:, 0:1], LN_EPS), w=["epsc"])
    op("dve", lambda e: e.memset(epsc[:, 1:2], RMS_EPS), w=["epsc"])
    op("dve", lambda e: e.memset(Sst[:], 0.0), w=["Sst"])

    def load_cols(src_ap, nrows, dst):
        op("sp", lambda e: e.dma_start(out=colraw[0:nrows, :], in_=src_ap), w=["colraw"], dma=True)
        b = bank("b")
        tp(psum[b][:, 0:nrows], colraw[0:nrows, :], r=["colraw"], w=[PK(b)], n=nrows)
        op("dve", lambda e: e.tensor_copy(out=dst, in_=psum[b][:, 0:nrows]), r=[PK(b)], w=["cols"])

    load_cols(lng_d.rearrange("r (c p) -> (r c) p", p=128), 96, gcol[:])
    load_cols(lnb_d.rearrange("r (c p) -> (r c) p", p=128), 96, bcol[:])
    op("sp", lambda e: e.dma_start(out=colraw[0:16, :], in_=hglb_d.rearrange("l (h p) -> (l h) p", p=128)), w=["colraw"], dma=True, semkey="colraw_a")
    op("sp", lambda e: e.dma_start(out=colraw[16:32, :], in_=hgn_d.rearrange("l (h p) -> (l h) p", p=128)), r=["colraw"], w=["colraw2"], dma=True, semkey="colraw_b")
    _b = bank("b")
    tp(psum[_b][:, 0:32], colraw[0:32, :], r=["colraw", "colraw2"], w=[PK(_b)], n=32)
    op("dve", lambda e: e.tensor_copy(out=hcol[:], in_=psum[_b][:, 0:32]), r=[PK(_b)], w=["cols"])
    ts(agcol[:], gcol[:], ALPHA, None, ALU.mult, None, r=["cols"], w=["acols"])
    ts(abcol[:], bcol[:], ALPHA, None, ALU.mult, None, r=["cols"], w=["acols"])

    def lower_bounds(src, dst_lb, dst_oml, dst_noml, e_ap, s_ap, r, w):
        act(e_ap, src, AF.Exp, r=r, w=w)
        tt(s_ap, e_ap[:, 0], e_ap[:, 1], ALU.add, r=w, w=w)
        tt(s_ap, s_ap, e_ap[:, 2], ALU.add, r=w, w=w)
        tt(s_ap, s_ap, e_ap[:, 3], ALU.add, r=w, w=w)
        op("dve", lambda e: e.reciprocal(out=s_ap, in_=s_ap), r=w, w=w)
        for l in range(1, 4):
            tt(e_ap[:, l], e_ap[:, l], s_ap, ALU.mult, r=w, w=w)
        op("dve", lambda e: e.memset(e_ap[:, 0], 0.0), r=w, w=w)
        tt(e_ap[:, 2], e_ap[:, 2], e_ap[:, 1], ALU.add, r=w, w=w)
        tt(e_ap[:, 3], e_ap[:, 3], e_ap[:, 2], ALU.add, r=w, w=w)

    lower_bounds(hcol[:, 0:16].rearrange("p (l h) -> p l h", l=4), None, None, None,
                 lbc[:, 1], lbc[:, 0, 0], r=["cols"], w=["lbc"])
    ts(lbc[:, 2], lbc[:, 1], -1.0, 1.0, ALU.mult, ALU.add, r=["lbc"], w=["lbc"])
    ts(lbc[:, 3], lbc[:, 1], 1.0, -1.0, ALU.mult, ALU.add, r=["lbc"], w=["lbc"])

    raw, rk = scr_f32(0, 8)
    raw3 = raw.rearrange("p (a n) -> p a n", a=4)
    op("sp", lambda e: e.dma_start(out=raw, in_=hglb_d.rearrange("(o l) n -> o (l n)", o=1).partition_broadcast(128)), w=rk, dma=True, semkey="rowld0")
    lower_bounds(raw3, None, None, None, raw3, tmp[:, 0, :], r=rk + [("tmp", 0)], w=rk + [("tmp", 0)])
    op("sp", lambda e: e.dma_start(out=lbrow_d.rearrange("(o l) n -> o (l n)", o=1), in_=raw[0:1, :]), r=rk, w=["lbrow_d"], dma=True, semkey="rowst0")

    stat_q = []
    pend = []

    def flush_stats():
        for f_ in stat_q:
            f_()
        del stat_q[:]

    def drain(n=None):
        k = len(pend) if n is None else min(n, len(pend))
        for _ in range(k):
            pend.pop(0)()

    def resid_chunk(c, tb, b, first, last):
        sl = slice(tb * TB, (tb + 1) * TB)
        xk = ("xa", c, tb)
        stt(xa[:, c, sl], psum[b][:], 0.5, xa[:, c, sl], ALU.mult, ALU.add, r=[PK(b), xk], w=[xk])
        i0 = tbf_next()
        i1 = tbf_next()
        act(tbf[:, i0, :], xa[:, c, sl], AF.Identity, r=[xk], w=[("tbf", i0)])
        act(tbf[:, i1, :], xa[:, c, sl], AF.Square, r=[xk], w=[("tbf", i1)])
        stat_q.append(lambda: mm(psum[6][:], ones_bf[:], tbf[:, i0, :], first, last, r=["ones_bf", ("tbf", i0)], w=[PK(6)]))
        stat_q.append(lambda: mm(psum[7][:], ones_bf[:], tbf[:, i1, :], first, last, r=["ones_bf", ("tbf", i1)], w=[PK(7)]))

    def ln_stats(tb):
        im, iv = tmp_next(), tmp_next()
        mean, var = tmp[:, im, :], tmp[:, iv, :]
        km, kv = ("tmp", im), ("tmp", iv)
        rstd, nmr = lnst[:, tb, 0, :], lnst[:, tb, 1, :]
        kr, kn = ("lnst", tb, 0), ("lnst", tb, 1)
        act(mean, psum[6][:], AF.Identity, r=[PK(6)], w=[km], scale=1.0 / D)
        tt(var, mean, mean, ALU.mult, r=[km], w=[kv])
        stt(var, psum[7][:], 1.0 / D, var, ALU.mult, ALU.subtract, r=[PK(7), kv], w=[kv])
        act(var, var, AF.Ln, r=[kv, "epsc"], w=[kv], bias=epsc[:, 0:1], scale=1.0)
        act(rstd, var, AF.Exp, r=[kv], w=[kr], scale=-0.5)
        stt(nmr, mean, -1.0, rstd, ALU.mult, ALU.mult, r=[km, kr], w=[kn])

    def ln_chunk(tb, c, col0, final):
        sl = slice(tb * TB, (tb + 1) * TB)
        rstd, nmr = lnst[:, tb, 0, :], lnst[:, tb, 1, :]
        kr, kn = ("lnst", tb, 0), ("lnst", tb, 1)
        xk = ("xa", c, tb)
        it_ = tmp_next()
        t = tmp[:, it_, :]
        kt = ("tmp", it_)
        tt(t, xa[:, c, sl], rstd, ALU.mult, r=[xk, kr], w=[kt])
        tt(t, t, nmr, ALU.add, r=[kt, kn], w=[kt])
        ci = col0 + c
        if final:
            act(xa[:, c, sl], t, AF.Identity, r=[kt, "cols"], w=[xk], scale=gcol[:, ci:ci + 1], bias=bcol[:, ci:ci + 1])
        else:
            act(xa[:, c, sl], t, AF.Identity, r=[kt, "acols"], w=[xk], scale=agcol[:, ci:ci + 1], bias=abcol[:, ci:ci + 1])
            act(xb[:, c, sl], t, AF.Identity, r=[kt, "cols"], w=[("xb", c, tb)], scale=gcol[:, ci:ci + 1], bias=bcol[:, ci:ci + 1])

    def ln_begin(tb, col0, final):
        flush_stats()
        ln_stats(tb)
        for c in range(KC):
            pend.append(lambda c=c: ln_chunk(tb, c, col0, final))

    XB = lambda tb: [("xb", c, tb) for c in range(KC)]

    def ffn(l, which, lnidx, final):
        w13 = w13_d[which][l].rearrange("(kc p) n -> p kc n", p=128)
        w2 = w2_d[which][l].rearrange("(j p) n -> p j n", p=128)
        order = [(0, 0), (1, 0), (2, 0), (0, 1), (1, 1), (2, 1)] + [(g, tb) for g in range(3, 11) for tb in range(2)]
        bufs = {}
        for n_it, (g, tb) in enumerate(order):
            if g not in bufs:
                ia, ib = wk_next(), wk_next()
                load_wk(ia, w13[:, :, g * 256:(g + 1) * 256])
                load_wk(ib, w13[:, :, DFF + g * 256: DFF + (g + 1) * 256])
                bufs[g] = (ia, ib)
            ia, ib = bufs[g]
            if n_it == 3:
                drain()
            sl = slice(tb * TB, (tb + 1) * TB)
            for jj in range(2):
                j = 2 * g + jj
                pa, pb = bank("a"), bank("a")
                for kc in range(KC):
                    mm(psum[pa][:], wk[ia][:, kc, jj * 128:(jj + 1) * 128], xb[:, kc, sl], kc == 0, kc == KC - 1,
                       r=[("wk", ia), ("xb", kc, tb)], w=[PK(pa)])
                for kc in range(KC):
                    mm(psum[pb][:], wk[ib][:, kc, jj * 128:(jj + 1) * 128], xb[:, kc, sl], kc == 0, kc == KC - 1,
                       r=[("wk", ib), ("xb", kc, tb)], w=[PK(pb)])
                it_ = tmp_next()
                act(tmp[:, it_, :], psum[pa][:], AF.Silu, r=[PK(pa)], w=[("tmp", it_)])
                hap, hk = hT(j, tb)
                tt(hap, tmp[:, it_, :], psum[pb][:], ALU.mult, r=[("tmp", it_), PK(pb)], w=hk)
            if n_it < 3:
                drain(3)
        nld = [0]
        for tb in range(2):
            for c in range(KC):
                ib = nld[0] % 2
                nld[0] += 1
                op("pool", lambda e, ib=ib, c=c: e.dma_start(out=w2b[ib][:], in_=w2[:, :, c * 128:(c + 1) * 128]),
                   w=[("w2b", ib)], dma=True)
                b = bank("b")
                for j in range(NJ):
                    hap, hk = hT(j, tb)
                    mm(psum[b][:], w2b[ib][:, j, :], hap, j == 0, j == NJ - 1, r=[("w2b", ib)] + hk, w=[PK(b)])
                flush_stats()
                resid_chunk(c, tb, b, c == 0, c == KC - 1)
                drain(1)
            drain()
            ln_begin(tb, lnidx * 8, final)

    def gelu_from_psum(b, out_ap, out_keys):
        ih, iw = tmp_next(), tmp_next()
        xh, wv = tmp[:, ih, :], tmp[:, iw, :]
        kh_, kw_ = ("tmp", ih), ("tmp", iw)
        act(xh, psum[b][:], AF.Identity, r=[PK(b)], w=[kh_], scale=0.5)
        act(wv, psum[b][:], AF.Square, r=[PK(b)], w=[kw_])
        ts(wv, wv, GC2, 1.0, ALU.mult, ALU.add, r=[kw_], w=[kw_])
        tt(wv, wv, xh, ALU.mult, r=[kw_, kh_], w=[kw_])
        act(wv, wv, AF.Tanh, r=[kw_], w=[kw_], scale=2.0 * GC1)
        stt(out_ap, wv, 1.0, xh, ALU.add, ALU.mult, r=[kw_, kh_], w=out_keys)

    def mixer_setup(l):
        op("sp", lambda e: e.dma_start(out=rows[:, 0, :], in_=lbrow_d[l:l + 1, :].partition_broadcast(128)), r=["lbrow_d"], w=["rows01"], dma=True, semkey="rowld")
        ts(rows[:, 1, :], rows[:, 0, :], -1.0, 1.0, ALU.mult, ALU.add, r=["rows01"], w=["rows1"])
        op("sp", lambda e: e.dma_start(out=rows[:, 2, :], in_=sglg_d[l:l + 1, :].partition_broadcast(128)), w=["rows2"], dma=True)
        op("sp", lambda e: e.dma_start(out=rows[:, 3, :], in_=sglb_d[l:l + 1, :].partition_broadcast(128)), w=["rows3"], dma=True)
        op("sp", lambda e: e.dma_start(out=bsrow[:], in_=sgbs_d[l:l + 1, :]), w=["bsrow"], dma=True)
        i0 = tmp_next()
        wraw = tmp[:, i0, :].rearrange("p (g s) -> p g s", g=4)
        op("sp", lambda e: e.dma_start(out=wraw, in_=sgws_d[l].rearrange("g t s -> t g s")), w=[("tmp", i0)], dma=True, semkey="wsld")
        b = bank("b")
        for g in range(4):
            tp(psum[b][:, g * 128:(g + 1) * 128], wraw[:, g, :], r=[("tmp", i0)], w=[PK(b)])
        tt(wsT[:], psum[b][:].rearrange("p (g t) -> p g t", g=4), cst[:, 6:10, :], ALU.mult, r=[PK(b), "cst"], w=["wsT"])

    def proj_fm(iw, col, tb, b):
        sl = slice(tb * TB, (tb + 1) * TB)
        for kc in range(KC):
            mm(psum[b][:], wk[iw][:, kc, col:col + 128], xb[:, kc, sl], kc == 0, kc == KC - 1,
               r=[("wk", iw), ("xb", kc, tb)], w=[PK(b)])

    def proj_tm(iw0, iw1, tb, t4, b):
        c0 = tb * TB + t4 * 128
        for half, iw in enumerate((iw0, iw1)):
            for kc in range(KC):
                mm(psum[b][:, half * 256:(half + 1) * 256], xb[:, kc, c0:c0 + 128], wk[iw][:, kc, :], kc == 0, kc == KC - 1,
                   r=[("wk", iw), ("xb", kc, tb)], w=[PK(b)])

    def mixer(l, hf):
        win = win_d[l].rearrange("(kc p) n -> p kc n", p=128)
        wa = wa_d[l].rearrange("(kc p) n -> p kc n", p=128)
        wbb = wbb_d[l].rearrange("(kc p) n -> p kc n", p=128)
        wo = wo_d[l].rearrange("(kc p) n -> p kc n", p=128)
        mixer_setup(l)

        def load_in(col0):
            i0, i1 = wk_next(), wk_next()
            load_wk(i0, win[:, :, col0:col0 + 256])
            load_wk(i1, win[:, :, col0 + 256:col0 + 512])
            return i0, i1

        for tb in range(2):
            sl = slice(tb * TB, (tb + 1) * TB)
            wg = load_in(1536)
            for h in range(4):
                b = bank("a")
                proj_fm(wg[h // 2], (h % 2) * 128, tb, b)
                it_ = tmp_next()
                act(tmp[:, it_, :], psum[b][:], AF.Silu, r=[PK(b)], w=[("tmp", it_)])
                ts(sgT[:, h, :], tmp[:, it_, :], hcol[:, 16 + l * 4 + h:16 + l * 4 + h + 1], None, ALU.mult, None,
                   r=[("tmp", it_), "cols"], w=sgT_k)
                drain(1)
            wu = load_in(2048)
            for g in range(4):
                b = bank("a")
                proj_fm(wu[g // 2], (g % 2) * 128, tb, b)
                gelu_from_psum(b, guT[:, g, :], guT_k)
                drain(1)
            wv_ = load_in(2560)
            for t4 in range(4):
                b = bank("a")
                proj_tm(wv_[0], wv_[1], tb, t4, b)
                gelu_from_psum(b, gv[:, t4, :], gv_k)
                op("dve", lambda e, t4=t4: e.bn_stats(out=small[:, t4 * 6:(t4 + 1) * 6], in_=gv[:, t4, :]), r=gv_k, w=["small_bn"])
                op("dve", lambda e, t4=t4: e.bn_aggr(out=small[:, 24 + t4 * 2:24 + t4 * 2 + 2], in_=small[:, t4 * 6:(t4 + 1) * 6]),
                   r=["small_bn"], w=["small_mv"])
            drain()
            mvv = small[:, 24:32].rearrange("p (t two) -> p t two", two=2)
            act(small[:, 32:36], mvv[:, :, 1], AF.Ln, r=["small_mv", "epsc"], w=["small_rs"], bias=epsc[:, 0:1], scale=1.0)
            act(small[:, 32:36], small[:, 32:36], AF.Exp, r=["small_rs"], w=["small_rs"], scale=-0.5)
            stt(small[:, 36:40], mvv[:, :, 0], -1.0, small[:, 32:36], ALU.mult, ALU.mult, r=["small_mv", "small_rs"], w=["small_rs"])
            for t4 in range(4):
                it_ = tmp_next()
                t = tmp[:, it_, :]
                act(t, gv[:, t4, :], AF.Identity, r=gv_k + ["small_rs"], w=[("tmp", it_)],
                    scale=small[:, 32 + t4:33 + t4], bias=small[:, 36 + t4:37 + t4])
                tt(t, t, rows[:, 2, :], ALU.mult, r=[("tmp", it_), "rows2"], w=[("tmp", it_)])
                tt(vn[:, t4, :], t, rows[:, 3, :], ALU.add, r=[("tmp", it_), "rows3"], w=vn_k)
            wf = load_in(512)
            fk = []
            for t4 in range(4):
                b = bank("a")
                proj_tm(wf[0], wf[1], tb, t4, b)
                ie, ik = tmp_next(), tmp_next()
                e_, k_ = tmp[:, ie, :], tmp[:, ik, :]
                ke, kk = ("tmp", ie), ("tmp", ik)
                act(e_, psum[b][:], AF.Exp, r=[PK(b)], w=[ke], scale=-1.0)
                ts(e_, e_, 1.0, None, ALU.add, None, r=[ke], w=[ke])
                op("dve", lambda e, e_=e_: e.reciprocal(out=e_, in_=e_), r=[ke], w=[ke])
                tt(e_, e_, rows[:, 1, :], ALU.mult, r=[ke, "rows1"], w=[ke])
                tt(e_, e_, rows[:, 0, :], ALU.add, r=[ke, "rows01"], w=[ke])
                act(lf[:, t4, :], e_, AF.Ln, r=[ke], w=lf_k)
                ts(kh[:, t4, :], e_, -1.0, 1.0, ALU.mult, ALU.add, r=[ke], w=kh_k)
            wi = load_in(1024)
            for t4 in range(4):
                b = bank("a")
                proj_tm(wi[0], wi[1], tb, t4, b)
                act(it[:, t4, :], psum[b][:], AF.Identity, r=[PK(b)], w=it_k)
            for t4 in range(4):
                b2 = bank("b")
                mm(psum[b2][:], cst[:, 1, :], lf[:, t4, :], True, True, r=["cst"] + lf_k, w=[PK(b2)])
                ie = tmp_next()
                e_ = tmp[:, ie, :]
                ke = ("tmp", ie)
                act(e_, psum[b2][:], AF.Exp, r=[PK(b2)], w=[ke])
                tt(kh[:, t4, :], kh[:, t4, :], e_, ALU.mult, r=kh_k + [ke], w=kh_k)
            wq = load_in(0)
            for h in range(4):
                b = bank("a")
                proj_fm(wf[h // 2], (h % 2) * 128, tb, b)
                ie, ibt, ieq, iek = tmp_next(), tmp_next(), tmp_next(), tmp_next()
                e_, bTs, eq, ek = tmp[:, ie, :], tmp[:, ibt, :], tmp[:, ieq, :], tmp[:, iek, :]
                ke, kbt, keq, kek = ("tmp", ie), ("tmp", ibt), ("tmp", ieq), ("tmp", iek)
                act(e_, psum[b][:], AF.Exp, r=[PK(b)], w=[ke], scale=-1.0)
                ts(e_, e_, 1.0, None, ALU.add, None, r=[ke], w=[ke])
                op("dve", lambda e, e_=e_: e.reciprocal(out=e_, in_=e_), r=[ke], w=[ke])
                ts(e_, e_, lbc[:, 3, l, h:h + 1], lbc[:, 2, l, h:h + 1], ALU.mult, ALU.add, r=[ke, "lbc"], w=[ke])
                b2 = bank("b")
                for t4 in range(4):
                    mm(psum[b2][:, t4 * 128:(t4 + 1) * 128], lf[:, t4, h * 128:(h + 1) * 128], cst[:, 2, :], True, True,
                       r=lf_k + ["cst"], w=[PK(b2)])
                act(bTs, psum[b2][:], AF.Identity, r=[PK(b2)], w=[kbt])
                dk = ("dec", h)
                bT3 = bTs.rearrange("p (c n) -> p c n", n=64)
                ts(dec[:, h, 0, :], bT3[:, :, 31], -1.0, None, ALU.mult, None, r=[kbt], w=[dk])
                act(dec[:, h, 1, :], bT3[:, :, 31], AF.Exp, r=[kbt], w=[dk])
                act(dec[:, h, 2, :], bT3[:, :, 63], AF.Exp, r=[kbt], w=[dk])
                tt(eq.rearrange("p (c n) -> p c n", n=64), bT3, dec[:, h, 0, :].unsqueeze(2).to_broadcast([128, 8, 64]), ALU.add,
                   r=[kbt, dk], w=[keq])
                act(ek, eq, AF.Exp, r=[keq], w=[kek], scale=-1.0)
                act(eq, eq, AF.Exp, r=[keq], w=[keq])
                tt(ktT[:, h, :], e_, ek, ALU.mult, r=[ke, kek], w=[("ktT", h)])
                b3 = bank("a")
                proj_fm(wq[h // 2], (h % 2) * 128, tb, b3)
                tt(qtT[:, h, :], psum[b3][:], eq, ALU.mult, r=[PK(b3), keq], w=[("qtT", h)])
            for t4 in range(4):
                ts_ = slice(t4 * 128, (t4 + 1) * 128)
                bp = bank("b")
                for h in range(4):
                    mm(psum[bp][:, h * 128:(h + 1) * 128], ktT[:, h, ts_], qtT[:, h, ts_], True, True,
                       r=[("ktT", h), ("qtT", h)], w=[PK(bp)])
                ip = t4 % 2
                tt(PTs[:, ip, :].rearrange("p (h t) -> p h t", h=4), psum[bp][:].rearrange("p (h t) -> p h t", h=4), cst[:, 2:6, :],
                   ALU.mult, r=[PK(bp), "cst"], w=[("PTs", ip)])
                bo = 6
                for h in range(4):
                    dk = ("dec", h)
                    Sh = Sst[:, l, h, :]
                    sk = ("S", l, h)
                    ck0 = 2 * t4
                    hs = slice(h * 128, (h + 1) * 128)
                    ts(Smb[:, 0, h, :], Sh, dec[:, h, 1, ck0:ck0 + 1], None, ALU.mult, None, r=[sk, dk], w=[("Smb", 0, h)])
                    mm(psum[bo][:, hs], it[:, t4, hs], PTs[:, ip, hs], True, False, r=it_k + [("PTs", ip)], w=[PK(bo)], sgc=True)
                    mm(psum[bo][:, h * 128:h * 128 + 64], Smb[:, 0, h, :], qtT[:, h, t4 * 128:t4 * 128 + 64], False, False,
                       r=[("Smb", 0, h), ("qtT", h)], w=[PK(bo)], sgc=True)
                    bs_ = bank("a")
                    mm(psum[bs_][:, 0:128], kh[0:64, t4, hs], it[0:64, t4, hs], True, True, r=kh_k + it_k, w=[PK(bs_)])
                    stt(Sh, Sh, dec[:, h, 2, ck0:ck0 + 1], psum[bs_][:, 0:128], ALU.mult, ALU.add, r=[sk, dk, PK(bs_)], w=[sk])
                    ts(Smb[:, 1, h, :], Sh, dec[:, h, 1, ck0 + 1:ck0 + 2], None, ALU.mult, None, r=[sk, dk], w=[("Smb", 1, h)])
                    mm(psum[bo][:, h * 128 + 64:h * 128 + 128], Smb[:, 1, h, :], qtT[:, h, t4 * 128 + 64:t4 * 128 + 128], False, True,
                       r=[("Smb", 1, h), ("qtT", h)], w=[PK(bo)], sgc=True)
                    bs2 = bank("a")
                    mm(psum[bs2][:, 0:128], kh[64:128, t4, hs], it[64:128, t4, hs], True, True, r=kh_k + it_k, w=[PK(bs2)])
                    stt(Sh, Sh, dec[:, h, 2, ck0 + 1:ck0 + 2], psum[bs2][:, 0:128], ALU.mult, ALU.add, r=[sk, dk, PK(bs2)], w=[sk])
                iq = tbf_next()
                act(tbf[:, iq, :], psum[bo][:], AF.Square, r=[PK(bo)], w=[("tbf", iq)])
                mm(psum[7][:], ones_bf[:], tbf[:, iq, :], True, True, r=["ones_bf", ("tbf", iq)], w=[PK(7)])
                ir = tmp_next()
                rs = tmp[:, ir, :]
                act(rs, psum[7][:], AF.Ln, r=[PK(7), "epsc"], w=[("tmp", ir)], bias=epsc[:, 1:2], scale=1.0 / 128)
                act(rs, rs, AF.Exp, r=[("tmp", ir)], w=[("tmp", ir)], scale=-0.5)
                tt(rs, psum[bo][:], rs, ALU.mult, r=[PK(bo), ("tmp", ir)], w=[("tmp", ir)])
                tt(ogT[:, :, ts_], rs.rearrange("p (h t) -> p h t", h=4), sgT[:, :, ts_], ALU.mult, r=[("tmp", ir)] + sgT_k, w=og_k)
            for g in range(4):
                b = bank("b")
                for t4 in range(4):
                    ts_ = slice(t4 * 128, (t4 + 1) * 128)
                    mm(psum[b][:, ts_], vn[:, t4, g * 128:(g + 1) * 128], wsT[:, g, :], True, False, r=vn_k + ["wsT"], w=[PK(b)])
                    mm(psum[b][:, ts_], ones_f[0:1, :], bsrow[0:1, g * 128:(g + 1) * 128], False, True, r=["ones_f", "bsrow"], w=[PK(b)])
                tt(hbT[:, g, :], psum[b][:], guT[:, g, :], ALU.mult, r=[PK(b)] + guT_k, w=hb_k)
            v5 = lambda t: t[:].rearrange("p k n -> p (k n)").rearrange("p (a k n) -> p a k n", a=2, k=4)
            for c2 in range(4):
                iab = wk_next()
                load_wk(iab, wa[:, :, c2 * 256:(c2 + 1) * 256], view=lambda t: v5(t)[:, 0])
                load_wk(iab, wbb[:, :, c2 * 256:(c2 + 1) * 256], view=lambda t: v5(t)[:, 1])
                iga, igb = wk_next(), wk_next()
                load_wk(iga, win[:, :, 3072 + c2 * 256:3072 + (c2 + 1) * 256])
                load_wk(igb, win[:, :, 4096 + c2 * 256:4096 + (c2 + 1) * 256])
                for cc in range(2):
                    c = 2 * c2 + cc
                    pya, pza, pyb, pzb = (0, 1, 2, 3) if c % 2 == 0 else (4, 5, 6, 7)
                    wav = v5(wk[iab])[:, 0]
                    wbv = v5(wk[iab])[:, 1]
                    co = cc * 128
                    for k4 in range(4):
                        mm(psum[pya][:], wav[:, k4, co:co + 128], ogT[:, k4, :], k4 == 0, k4 == 3, r=[("wk", iab)] + og_k, w=[PK(pya)])
                    proj_fm(iga, cc * 128, tb, pza)
                    for k4 in range(4):
                        mm(psum[pyb][:], wbv[:, k4, co:co + 128], hbT[:, k4, :], k4 == 0, k4 == 3, r=[("wk", iab)] + hb_k, w=[PK(pyb)])
                    proj_fm(igb, cc * 128, tb, pzb)
                    i1, i2 = tmp_next(), tmp_next()
                    t1, t2 = tmp[:, i1, :], tmp[:, i2, :]
                    act(t1, psum[pza][:], AF.Tanh, r=[PK(pza)], w=[("tmp", i1)], scale=0.5)
                    act(t2, psum[pzb][:], AF.Tanh, r=[PK(pzb)], w=[("tmp", i2)], scale=0.5)
                    stt(t1, t1, 1.0, psum[pya][:], ALU.add, ALU.mult, r=[("tmp", i1), PK(pya)], w=[("tmp", i1)])
                    stt(t2, t2, 1.0, psum[pyb][:], ALU.add, ALU.mult, r=[("tmp", i2), PK(pyb)], w=[("tmp", i2)])
                    tt(yT[:, c, :], t1, t2, ALU.add, r=[("tmp", i1), ("tmp", i2)], w=[yT_k[c]])
            for c2 in range(4):
                iw = wk_next()
                load_wk(iw, wo[:, :, c2 * 256:(c2 + 1) * 256])
                for cc in range(2):
                    c = 2 * c2 + cc
                    b = bank("b")
                    for kc in range(KC):
                        mm(psum[b][:], wk[iw][:, kc, cc * 128:(cc + 1) * 128], yT[:, kc, :], kc == 0, kc == KC - 1,
                           r=[("wk", iw), yT_k[kc]], w=[PK(b)])
                    flush_stats()
                    resid_chunk(c, tb, b, c == 0, c == KC - 1)
            drain()
            ln_begin(tb, (l * 3 + 1) * 8, False)

    dec = sb("dec", [128, 4, 3, 8], F32)

    def load_x(hf):
        xin = x_d[hf * NT:(hf + 1) * NT, :].rearrange("(t p) d -> p t d", p=128)
        for t8 in range(8):
            st_ap, st_k = scr_f32(t8 * 4, 4)
            op("sp", lambda e, st_ap=st_ap, t8=t8: e.dma_start(out=st_ap, in_=xin[:, t8, :]), w=st_k, dma=True, semkey=("xst", t8))
        for c in range(KC):
            for tb in range(2):
                b = bank("a")
                rk = []
                for i in range(4):
                    t8 = tb * 4 + i
                    st_ap, st_k = scr_f32(t8 * 4, 4)
                    tp(psum[b][:, i * 128:(i + 1) * 128], st_ap[:, c * 128:(c + 1) * 128], r=st_k, w=[PK(b)])
                sl = slice(tb * TB, (tb + 1) * TB)
                act(xa[:, c, sl], psum[b][:], AF.Identity, r=[PK(b)], w=[("xa", c, tb)], scale=ALPHA)
                op("dve", lambda e, c=c, sl=sl, b=b: e.tensor_copy(out=xb[:, c, sl], in_=psum[b][:]), r=[PK(b), ("xa", c, tb)], w=[("xb", c, tb)])

    def store_x(hf):
        xout = out_d[hf * NT:(hf + 1) * NT, :].rearrange("(t p) d -> p t d", p=128)
        for t8 in range(8):
            tb, i = t8 // 4, t8 % 4
            st_ap, st_k = scr_f32(t8 * 4, 4)
            for half in range(2):
                b = bank("a")
                for cc in range(4):
                    c = half * 4 + cc
                    tp(psum[b][:, cc * 128:(cc + 1) * 128], xa[:, c, t8 * 128:(t8 + 1) * 128], r=[("xa", c, tb)], w=[PK(b)])
                if half == 0:
                    act(st_ap[:, 0:512], psum[b][:], AF.Identity, r=[PK(b)], w=st_k)
                else:
                    op("dve", lambda e, st_ap=st_ap, b=b: e.tensor_copy(out=st_ap[:, 512:1024], in_=psum[b][:]), r=[PK(b)], w=st_k)
            op("sp", lambda e, st_ap=st_ap, t8=t8: e.dma_start(out=xout[:, t8, :], in_=st_ap), r=st_k, w=[("out", hf, t8)], dma=True,
               semkey=("ost", t8))

    def dbg_dump(i):
        if debug:
            drain()
            op("sp", lambda e, i=i: e.dma_start(out=dbg_d[i], in_=xa[:]), r=[("xa", c, tb) for c in range(KC) for tb in range(2)],
               w=[("dbg", i)], dma=True, semkey=("dbg", i))

    for hf in range(nhalf):
        load_x(hf)
        for l in range(depth):
            last = (l == depth - 1)
            if STAGES >= 1:
                ffn(l, 0, l * 3 + 0, False)
            if hf == 0 and l == 0:
                dbg_dump(0)
            if STAGES >= 2:
                mixer(l, hf)
            if hf == 0 and l == 0:
                dbg_dump(1)
            if STAGES >= 3:
                ffn(l, 1, l * 3 + 2, last)
            if hf == 0 and l == 0:
                dbg_dump(2)
        drain()
        store_x(hf)
    S.emit()
    es.close()
    return nc


_CACHE = {}


def kernel(x, ffn1_w13, ffn1_w2, ffn2_w13, ffn2_w2, ln_g, ln_b, w_in, hg_lb, hg_norm_g,
           sg_ln_g, sg_ln_b, sg_ws, sg_bs, w_branch_a, w_branch_b, w_out):
    f = lambda a: np.ascontiguousarray(np.asarray(a, dtype=np.float32))
    x = f(x)
    B = x.shape[0]
    if "nc" not in _CACHE:
        _CACHE["nc"] = build_program(DEPTH, 2, False)
    nc = _CACHE["nc"]
    shared = dict(
        ffn1_w13=f(ffn1_w13), ffn1_w2=f(ffn1_w2), ffn2_w13=f(ffn2_w13), ffn2_w2=f(ffn2_w2),
        ln_g=f(ln_g).reshape(DEPTH * 3, D), ln_b=f(ln_b).reshape(DEPTH * 3, D), w_in=f(w_in),
        hg_lb=f(hg_lb), hg_norm_g=f(hg_norm_g), sg_ln_g=f(sg_ln_g), sg_ln_b=f(sg_ln_b),
        sg_ws=f(sg_ws), sg_bs=f(sg_bs).reshape(DEPTH, 512), w_branch_a=f(w_branch_a),
        w_branch_b=f(w_branch_b), w_out=f(w_out), consts=make_consts())
    in_maps = [dict(shared, x=x[b]) for b in range(B)]
    res = run_bass_kernel_spmd(nc, in_maps, core_ids=list(range(B)))
    return np.stack([np.asarray(r["out"], dtype=np.float32) for r in res.results], axis=0)
```

```python
import contextlib
import numpy as np
import concourse.bass as bass
import concourse.mybir as mybir
from concourse.bass_utils import run_bass_kernel_spmd

F32 = mybir.dt.float32
BF16 = mybir.dt.bfloat16
AF = mybir.ActivationFunctionType
ALU = mybir.AluOpType

D = 1024
KC = 8
DFF = 2816
NJ = 22
DEPTH = 4
NT = 1024
TB = 512
DIN = 5120
ALPHA = (2 * DEPTH) ** 0.25
LN_EPS = 1e-5
RMS_EPS = 1e-6
GC1 = 0.7978845608028654
GC2 = 0.044715
COMPUTE = ("pe", "act", "dve", "pool")
STAGES = 3
EMBED_WAITS = True


class Sched:
    def __init__(self, nc):
        self.nc = nc
        self.ops = []
        self.lastw = {}
        self.rd_eng = {}
        self.rd_dma = {}

    def op(self, eng, fn, r=(), w=(), dma=False, semkey=None):
        idx = len(self.ops)
        deps = set()
        for k in r:
            if k in self.lastw:
                deps.add(self.lastw[k])
        for k in w:
            if k in self.lastw:
                deps.add(self.lastw[k])
            deps.update(self.rd_eng.get(k, {}).values())
            deps.update(self.rd_dma.get(k, ()))
        for k in r:
            if dma:
                self.rd_dma.setdefault(k, []).append(idx)
            else:
                self.rd_eng.setdefault(k, {})[eng] = idx
        for k in w:
            self.lastw[k] = idx
            self.rd_eng[k] = {}
            self.rd_dma[k] = []
        if dma and semkey is None:
            semkey = w[0]
        self.ops.append(dict(eng=eng, fn=fn, deps=deps, dma=dma, semkey=semkey))
        return idx

    def emit(self):
        nc = self.nc
        ops = self.ops
        for i, o in enumerate(ops):
            best = {}
            dd = []
            for d in o["deps"]:
                p = ops[d]
                if p["dma"]:
                    dd.append(d)
                else:
                    if p["eng"] == "pe" and o["eng"] == "pe" and not o["dma"]:
                        continue
                    if d > best.get(p["eng"], -1):
                        best[p["eng"]] = d
            o["cdeps"] = list(best.values()) + dd
        needs = set()
        for o in ops:
            needs.update(o["cdeps"])
        stack = contextlib.ExitStack()
        eng_sem = {e: stack.enter_context(nc.semaphore("s_" + e)) for e in COMPUTE}
        dma_sem = {}
        eng_cnt = {e: 0 for e in COMPUTE}
        dma_cnt = {}
        for i, o in enumerate(ops):
            if o["dma"]:
                k = o["semkey"]
                if k not in dma_sem:
                    dma_sem[k] = stack.enter_context(nc.semaphore("d%d" % len(dma_sem)))
                    dma_cnt[k] = 0
                dma_cnt[k] += 16
                o["sig"] = (dma_sem[k], dma_cnt[k])
            elif i in needs:
                eng_cnt[o["eng"]] += 1
                o["sig"] = (eng_sem[o["eng"]], eng_cnt[o["eng"]])
            else:
                o["sig"] = None
        streams = {}
        for i, o in enumerate(ops):
            streams.setdefault(o["eng"], []).append(i)

        def run(ename, e):
            waited = {}
            for i in streams.get(ename, []):
                o = ops[i]
                need = {}
                for d in o["cdeps"]:
                    sem, val = ops[d]["sig"]
                    key = id(sem)
                    if waited.get(key, 0) >= val:
                        continue
                    if key not in need or need[key][1] < val:
                        need[key] = (sem, val)
                need = list(need.values())
                for sem, val in need[:-1]:
                    e.wait_ge(sem, val)
                    waited[id(sem)] = val
                ins = o["fn"](e)
                if need:
                    sem, val = need[-1]
                    if EMBED_WAITS:
                        ins._wait_ge(sem, val)
                    else:
                        raise RuntimeError("unreachable")
                    waited[id(sem)] = val
                if o["sig"] is not None:
                    ins.then_inc(o["sig"][0], 16 if o["dma"] else 1)
            last = {}
            for i in streams.get(ename, []):
                o = ops[i]
                if o["dma"]:
                    last[id(o["sig"][0])] = o["sig"]
            for sem, val in last.values():
                if waited.get(id(sem), 0) < val:
                    e.wait_ge(sem, val)

        with nc.Block() as block:
            @block.tensor
            def _(e):
                run("pe", e)

            @block.scalar
            def _(e):
                run("act", e)

            @block.vector
            def _(e):
                run("dve", e)

            @block.gpsimd
            def _(e):
                run("pool", e)

            @block.sync
            def _(e):
                run("sp", e)
        stack.close()


def make_consts():
    s = np.arange(128)[:, None]
    t = np.arange(128)[None, :]
    same = (s // 64) == (t // 64)
    ident = np.eye(128, dtype=np.float32)
    ublk = ((s <= t) & same).astype(np.float32)
    mgt = ((s > t) & same).astype(np.float32)
    triu = (s <= t).astype(np.float32)
    c = np.zeros((128, 10, 128), np.float32)
    c[:, 0] = ident
    c[:, 1] = mgt
    for i in range(4):
        c[:, 2 + i] = ublk
        c[:, 6 + i] = triu
    return c


def build_program(depth=DEPTH, nhalf=2, debug=False):
    nc = bass.Bass("TRN2", target_bir_lowering=False)
    T = nhalf * NT
    dr = lambda name, shape, kind="ExternalInput": nc.dram_tensor(name, list(shape), F32, kind=kind).ap()
    x_d = dr("x", [T, D])
    w13_d = [dr("ffn1_w13", [DEPTH, D, 2 * DFF]), dr("ffn2_w13", [DEPTH, D, 2 * DFF])]
    w2_d = [dr("ffn1_w2", [DEPTH, DFF, D]), dr("ffn2_w2", [DEPTH, DFF, D])]
    lng_d = dr("ln_g", [DEPTH * 3, D])
    lnb_d = dr("ln_b", [DEPTH * 3, D])
    win_d = dr("w_in", [DEPTH, D, DIN])
    hglb_d = dr("hg_lb", [DEPTH, 512])
    hgn_d = dr("hg_norm_g", [DEPTH, 512])
    sglg_d = dr("sg_ln_g", [DEPTH, 512])
    sglb_d = dr("sg_ln_b", [DEPTH, 512])
    sgws_d = dr("sg_ws", [DEPTH, 4, 128, 128])
    sgbs_d = dr("sg_bs", [DEPTH, 512])
    wa_d = dr("w_branch_a", [DEPTH, 512, D])
    wbb_d = dr("w_branch_b", [DEPTH, 512, D])
    wo_d = dr("w_out", [DEPTH, D, D])
    cst_d = dr("consts", [128, 10, 128])
    out_d = dr("out", [T, D], kind="ExternalOutput")
    dbg_d = dr("dbg", [3, 128, KC, NT], kind="ExternalOutput") if debug else None
    lbrow_d = nc.dram_tensor("lbrow_scratch", [DEPTH, 512], F32).ap()

    es = contextlib.ExitStack()
    sb = lambda name, shape, dt: es.enter_context(nc.sbuf_tensor(name, list(shape), dt))
    S = Sched(nc)
    op = S.op

    xa = sb("xa", [128, KC, NT], F32)
    xb = sb("xb", [128, KC, NT], BF16)
    scr = sb("scr", [128, NJ * NT], BF16)
    wk = [sb("wk%d" % i, [128, KC, 256], BF16) for i in range(8)]
    w2b = [sb("w2b%d" % i, [128, NJ, 128], BF16) for i in range(2)]
    tmp = sb("tmp", [128, 6, 512], F32)
    tbf = sb("tbf", [128, 4, 512], BF16)
    cst = sb("cst", [128, 10, 128], F32)
    ones_bf = sb("ones_bf", [128, 128], BF16)
    ones_f = sb("ones_f", [1, 128], F32)
    epsc = sb("epsc", [128, 3], F32)
    colraw = sb("colraw", [128, 128], F32)
    gcol = sb("gcol", [128, 96], F32)
    bcol = sb("bcol", [128, 96], F32)
    agcol = sb("agcol", [128, 96], F32)
    abcol = sb("abcol", [128, 96], F32)
    hcol = sb("hcol", [128, 32], F32)
    lbc = sb("lbc", [128, 4, 4, 4], F32)
    Sst = sb("Sst", [128, depth, 4, 128], F32)
    Smb = sb("Smb", [128, 2, 4, 128], BF16)
    rows = sb("rows", [128, 4, 512], F32)
    bsrow = sb("bsrow", [1, 512], F32)
    wsT = sb("wsT", [128, 4, 128], BF16)
    qtT = sb("qtT", [128, 4, 512], BF16)
    ktT = sb("ktT", [128, 4, 512], BF16)
    small = sb("small", [128, 64], F32)
    lnst = sb("lnst", [128, 2, 2, 512], F32)
    PTs = sb("PTs", [128, 2, 512], BF16)
    psum = [es.enter_context(nc.psum_tensor("ps%d" % i, [128, 512], F32)) for i in range(8)]
    PK = lambda b: ("ps", b)

    def scr_bf(off_blk, nblk):
        return scr[:, off_blk * 512:(off_blk + nblk) * 512], [("scr", b) for b in range(off_blk, off_blk + nblk)]

    def scr_f32(off_blk, nblk):
        ap, keys = scr_bf(off_blk, nblk)
        return ap.bitcast(F32), keys

    def hT(j, tb):
        return scr[:, j * NT + tb * TB: j * NT + (tb + 1) * TB], [("scr", j * 2 + tb)]

    sgT_ap, sgT_k = scr_f32(0, 8)
    guT_ap, guT_k = scr_f32(8, 8)
    gv_ap, gv_k = scr_f32(16, 8)
    lf_ap, lf_k = scr_f32(24, 8)
    vn_ap, vn_k = scr_bf(32, 4)
    kh_ap, kh_k = scr_bf(36, 4)
    it_ap, it_k = scr_bf(40, 4)
    v3 = lambda ap: ap.rearrange("p (a n) -> p a n", a=4)
    sgT, guT, gv, lf, vn, kh, it = map(v3, (sgT_ap, guT_ap, gv_ap, lf_ap, vn_ap, kh_ap, it_ap))
    yT_ap, yT_k = scr_bf(16, 8)
    hb_ap, hb_k = scr_bf(24, 4)
    og_ap, og_k = scr_bf(28, 4)
    yT = yT_ap.rearrange("p (a n) -> p a n", a=8)
    hbT = v3(hb_ap)
    ogT = v3(og_ap)

    bankrot = {"a": [0, [0, 1, 2, 3]], "b": [0, [4, 5]], "a6": [0, [0, 1, 2, 3, 4, 5]]}

    def bank(g):
        st = bankrot[g]
        b = st[1][st[0] % len(st[1])]
        st[0] += 1
        return b

    wkrot = [0]

    def wk_next():
        i = wkrot[0] % 8
        wkrot[0] += 1
        return i

    def load_wk(i, src, view=None):
        dst = wk[i][:] if view is None else view(wk[i])
        op("pool", lambda e: e.dma_start(out=dst, in_=src), w=[("wk", i)], dma=True)

    tmprot = [0]

    def tmp_next():
        i = tmprot[0] % 6
        tmprot[0] += 1
        return i

    tbfrot = [0]

    def tbf_next():
        i = tbfrot[0] % 4
        tbfrot[0] += 1
        return i

    def act(out, in_, func, r, w, **kw):
        op("act", lambda e: e.activation(out=out, in_=in_, func=func, **kw), r=r, w=w)

    def tt(out, in0, in1, alu, r, w, eng="dve"):
        op(eng, lambda e: e.tensor_tensor(out=out, in0=in0, in1=in1, op=alu), r=r, w=w)

    def ts(out, in0, s1, s2, op0, op1, r, w):
        if s2 is None:
            op("dve", lambda e: e.tensor_scalar(out=out, in0=in0, scalar1=s1, scalar2=None, op0=op0), r=r, w=w)
        else:
            op("dve", lambda e: e.tensor_scalar(out=out, in0=in0, scalar1=s1, scalar2=s2, op0=op0, op1=op1), r=r, w=w)

    def stt(out, in0, scalar, in1, op0, op1, r, w):
        op("dve", lambda e: e.scalar_tensor_tensor(out=out, in0=in0, scalar=scalar, in1=in1, op0=op0, op1=op1), r=r, w=w)

    def mm(out, lhsT, rhs, start, stop, r, w, sgc=False):
        if sgc:
            op("pe", lambda e: e.matmul(out, lhsT, rhs, start=start, stop=stop, skip_group_check=True), r=r, w=w)
        else:
            op("pe", lambda e: e.matmul(out, lhsT, rhs, start=start, stop=stop), r=r, w=w)

    def tp(out, in_, r, w, n=128):
        op("pe", lambda e: e.transpose(out, in_, cst[0:n, 0, 0:n]), r=list(r) + ["cst"], w=w)

    op("sp", lambda e: e.dma_start(out=cst[:], in_=cst_d), w=["cst"], dma=True)
    op("dve", lambda e: e.memset(ones_bf[:], 1.0), w=["ones_bf"])
    op("dve", lambda e: e.memset(ones_f[:], 1.0), w=["ones_f"])
    op("dve", lambda e: e.memset(epsc[:, 0:1], LN_EPS), w=["epsc"])
    op("dve", lambda e: e.memset(epsc[:, 1:2], RMS_EPS), w=["epsc"])
    op("dve", lambda e: e.memset(epsc[:, 2:3], 1.0), w=["epsc"])
    op("dve", lambda e: e.memset(Sst[:], 0.0), w=["Sst"])

    def load_cols(src_ap, nrows, dst):
        op("sp", lambda e: e.dma_start(out=colraw[0:nrows, :], in_=src_ap), w=["colraw"], dma=True)
        b = bank("b")
        tp(psum[b][:, 0:nrows], colraw[0:nrows, :], r=["colraw"], w=[PK(b)], n=nrows)
        op("dve", lambda e: e.tensor_copy(out=dst, in_=psum[b][:, 0:nrows]), r=[PK(b)], w=["cols"])

    load_cols(lng_d.rearrange("r (c p) -> (r c) p", p=128), 96, gcol[:])
    load_cols(lnb_d.rearrange("r (c p) -> (r c) p", p=128), 96, bcol[:])
    op("sp", lambda e: e.dma_start(out=colraw[0:16, :], in_=hglb_d.rearrange("l (h p) -> (l h) p", p=128)), w=["colraw"], dma=True, semkey="colraw_a")
    op("sp", lambda e: e.dma_start(out=colraw[16:32, :], in_=hgn_d.rearrange("l (h p) -> (l h) p", p=128)), r=["colraw"], w=["colraw2"], dma=True, semkey="colraw_b")
    _b = bank("b")
    tp(psum[_b][:, 0:32], colraw[0:32, :], r=["colraw", "colraw2"], w=[PK(_b)], n=32)
    op("dve", lambda e: e.tensor_copy(out=hcol[:], in_=psum[_b][:, 0:32]), r=[PK(_b)], w=["cols"])
    ts(agcol[:], gcol[:], ALPHA, None, ALU.mult, None, r=["cols"], w=["acols"])
    ts(abcol[:], bcol[:], ALPHA, None, ALU.mult, None, r=["cols"], w=["acols"])

    def lower_bounds(src, dst_lb, dst_oml, dst_noml, e_ap, s_ap, r, w):
        act(e_ap, src, AF.Exp, r=r, w=w)
        tt(s_ap, e_ap[:, 0], e_ap[:, 1], ALU.add, r=w, w=w)
        tt(s_ap, s_ap, e_ap[:, 2], ALU.add, r=w, w=w)
        tt(s_ap, s_ap, e_ap[:, 3], ALU.add, r=w, w=w)
        op("dve", lambda e: e.reciprocal(out=s_ap, in_=s_ap), r=w, w=w)
        for l in range(1, 4):
            tt(e_ap[:, l], e_ap[:, l], s_ap, ALU.mult, r=w, w=w)
        op("dve", lambda e: e.memset(e_ap[:, 0], 0.0), r=w, w=w)
        tt(e_ap[:, 2], e_ap[:, 2], e_ap[:, 1], ALU.add, r=w, w=w)
        tt(e_ap[:, 3], e_ap[:, 3], e_ap[:, 2], ALU.add, r=w, w=w)

    lower_bounds(hcol[:, 0:16].rearrange("p (l h) -> p l h", l=4), None, None, None,
                 lbc[:, 1], lbc[:, 0, 0], r=["cols"], w=["lbc"])
    ts(lbc[:, 2], lbc[:, 1], -1.0, 1.0, ALU.mult, ALU.add, r=["lbc"], w=["lbc"])
    ts(lbc[:, 3], lbc[:, 1], 1.0, -1.0, ALU.mult, ALU.add, r=["lbc"], w=["lbc"])

    raw, rk = scr_f32(0, 8)
    raw3 = raw.rearrange("p (a n) -> p a n", a=4)
    op("sp", lambda e: e.dma_start(out=raw, in_=hglb_d.rearrange("(o l) n -> o (l n)", o=1).partition_broadcast(128)), w=rk, dma=True, semkey="rowld0")
    lower_bounds(raw3, None, None, None, raw3, tmp[:, 0, :], r=rk + [("tmp", 0)], w=rk + [("tmp", 0)])
    op("sp", lambda e: e.dma_start(out=lbrow_d.rearrange("(o l) n -> o (l n)", o=1), in_=raw[0:1, :]), r=rk, w=["lbrow_d"], dma=True, semkey="rowst0")

    stat_q = []
    pend = []

    def flush_stats():
        for f_ in stat_q:
            f_()
        del stat_q[:]

    def drain(n=None):
        k = len(pend) if n is None else min(n, len(pend))
        for _ in range(k):
            pend.pop(0)()

    def resid_chunk(c, tb, b, first, last, sbk=(6, 7)):
        sl = slice(tb * TB, (tb + 1) * TB)
        xk = ("xa", c, tb)
        stt(xa[:, c, sl], psum[b][:], 0.5, xa[:, c, sl], ALU.mult, ALU.add, r=[PK(b), xk], w=[xk])
        i0 = tbf_next()
        i1 = tbf_next()
        act(tbf[:, i0, :], xa[:, c, sl], AF.Identity, r=[xk], w=[("tbf", i0)])
        act(tbf[:, i1, :], xa[:, c, sl], AF.Square, r=[xk], w=[("tbf", i1)])
        s1, s2 = sbk
        stat_q.append(lambda: mm(psum[s1][:], ones_bf[:], tbf[:, i0, :], first, last, r=["ones_bf", ("tbf", i0)], w=[PK(s1)]))
        stat_q.append(lambda: mm(psum[s2][:], ones_bf[:], tbf[:, i1, :], first, last, r=["ones_bf", ("tbf", i1)], w=[PK(s2)]))

    def ln_stats(tb, sbk=(6, 7)):
        s1, s2 = sbk
        im, iv = tmp_next(), tmp_next()
        mean, var = tmp[:, im, :], tmp[:, iv, :]
        km, kv = ("tmp", im), ("tmp", iv)
        rstd, nmr = lnst[:, tb, 0, :], lnst[:, tb, 1, :]
        kr, kn = ("lnst", tb, 0), ("lnst", tb, 1)
        act(mean, psum[s1][:], AF.Identity, r=[PK(s1)], w=[km], scale=1.0 / D)
        tt(var, mean, mean, ALU.mult, r=[km], w=[kv])
        stt(var, psum[s2][:], 1.0 / D, var, ALU.mult, ALU.subtract, r=[PK(s2), kv], w=[kv])
        act(var, var, AF.Ln, r=[kv, "epsc"], w=[kv], bias=epsc[:, 0:1], scale=1.0)
        act(rstd, var, AF.Exp, r=[kv], w=[kr], scale=-0.5)
        stt(nmr, mean, -1.0, rstd, ALU.mult, ALU.mult, r=[km, kr], w=[kn])

    def ln_chunk(tb, c, col0, final):
        sl = slice(tb * TB, (tb + 1) * TB)
        rstd, nmr = lnst[:, tb, 0, :], lnst[:, tb, 1, :]
        kr, kn = ("lnst", tb, 0), ("lnst", tb, 1)
        xk = ("xa", c, tb)
        it_ = tmp_next()
        t = tmp[:, it_, :]
        kt = ("tmp", it_)
        tt(t, xa[:, c, sl], rstd, ALU.mult, r=[xk, kr], w=[kt])
        tt(t, t, nmr, ALU.add, r=[kt, kn], w=[kt])
        ci = col0 + c
        if final:
            act(xa[:, c, sl], t, AF.Identity, r=[kt, "cols"], w=[xk], scale=gcol[:, ci:ci + 1], bias=bcol[:, ci:ci + 1])
        else:
            act(xa[:, c, sl], t, AF.Identity, r=[kt, "acols"], w=[xk], scale=agcol[:, ci:ci + 1], bias=abcol[:, ci:ci + 1])
            act(xb[:, c, sl], t, AF.Identity, r=[kt, "cols"], w=[("xb", c, tb)], scale=gcol[:, ci:ci + 1], bias=bcol[:, ci:ci + 1])

    def ln_begin(tb, col0, final, sbk=(6, 7), defer=True):
        flush_stats()
        ln_stats(tb, sbk)
        for c in range(KC):
            if defer:
                pend.append(lambda c=c: ln_chunk(tb, c, col0, final))
            else:
                ln_chunk(tb, c, col0, final)

    XB = lambda tb: [("xb", c, tb) for c in range(KC)]

    def ffn(l, which, lnidx, final):
        w13 = w13_d[which][l].rearrange("(kc p) n -> p kc n", p=128)
        w2 = w2_d[which][l].rearrange("(j p) n -> p j n", p=128)
        order = [(0, 0), (1, 0), (2, 0), (0, 1), (1, 1), (2, 1)] + [(g, tb) for g in range(3, 11) for tb in range(2)]
        bufs = {}
        for n_it, (g, tb) in enumerate(order):
            if g not in bufs:
                ia, ib = wk_next(), wk_next()
                load_wk(ia, w13[:, :, g * 256:(g + 1) * 256])
                load_wk(ib, w13[:, :, DFF + g * 256: DFF + (g + 1) * 256])
                bufs[g] = (ia, ib)
            ia, ib = bufs[g]
            if n_it == 3:
                drain()
            sl = slice(tb * TB, (tb + 1) * TB)
            for jj in range(2):
                j = 2 * g + jj
                pa, pb = bank("a"), bank("a")
                for kc in range(KC):
                    mm(psum[pa][:], wk[ia][:, kc, jj * 128:(jj + 1) * 128], xb[:, kc, sl], kc == 0, kc == KC - 1,
                       r=[("wk", ia), ("xb", kc, tb)], w=[PK(pa)])
                for kc in range(KC):
                    mm(psum[pb][:], wk[ib][:, kc, jj * 128:(jj + 1) * 128], xb[:, kc, sl], kc == 0, kc == KC - 1,
                       r=[("wk", ib), ("xb", kc, tb)], w=[PK(pb)])
                it_ = tmp_next()
                act(tmp[:, it_, :], psum[pa][:], AF.Silu, r=[PK(pa)], w=[("tmp", it_)])
                hap, hk = hT(j, tb)
                tt(hap, tmp[:, it_, :], psum[pb][:], ALU.mult, r=[("tmp", it_), PK(pb)], w=hk)
            if n_it < 3:
                drain(3)
        sbks = {0: (6, 7), 1: (2, 3)}
        for c in range(KC):
            ib = c % 2
            op("pool", lambda e, ib=ib, c=c: e.dma_start(out=w2b[ib][:], in_=w2[:, :, c * 128:(c + 1) * 128]),
               w=[("w2b", ib)], dma=True)
            for tb in range(2):
                b = bank("b")
                for j in range(NJ):
                    hap, hk = hT(j, tb)
                    mm(psum[b][:], w2b[ib][:, j, :], hap, j == 0, j == NJ - 1, r=[("w2b", ib)] + hk, w=[PK(b)])
                flush_stats()
                resid_chunk(c, tb, b, c == 0, c == KC - 1, sbks[tb])
                if c == KC - 1 and tb == 0:
                    ln_begin(0, lnidx * 8, final, sbks[0], defer=False)
                drain(1)
        drain()
        ln_begin(1, lnidx * 8, final, sbks[1], defer=True)

    def gelu_from_psum(b, out_ap, out_keys):
        ih, iw = tmp_next(), tmp_next()
        xh, wv = tmp[:, ih, :], tmp[:, iw, :]
        kh_, kw_ = ("tmp", ih), ("tmp", iw)
        act(xh, psum[b][:], AF.Identity, r=[PK(b)], w=[kh_], scale=0.5)
        act(wv, psum[b][:], AF.Square, r=[PK(b)], w=[kw_])
        ts(wv, wv, GC2, 1.0, ALU.mult, ALU.add, r=[kw_], w=[kw_])
        tt(wv, wv, xh, ALU.mult, r=[kw_, kh_], w=[kw_])
        act(wv, wv, AF.Tanh, r=[kw_], w=[kw_], scale=2.0 * GC1)
        stt(out_ap, wv, 1.0, xh, ALU.add, ALU.mult, r=[kw_, kh_], w=out_keys)

    def mixer_setup(l):
        op("sp", lambda e: e.dma_start(out=rows[:, 0, :], in_=lbrow_d[l:l + 1, :].partition_broadcast(128)), r=["lbrow_d"], w=["rows01"], dma=True, semkey="rowld")
        ts(rows[:, 1, :], rows[:, 0, :], -1.0, 1.0, ALU.mult, ALU.add, r=["rows01"], w=["rows1"])
        op("sp", lambda e: e.dma_start(out=rows[:, 2, :], in_=sglg_d[l:l + 1, :].partition_broadcast(128)), w=["rows2"], dma=True)
        op("sp", lambda e: e.dma_start(out=rows[:, 3, :], in_=sglb_d[l:l + 1, :].partition_broadcast(128)), w=["rows3"], dma=True)
        op("sp", lambda e: e.dma_start(out=bsrow[:], in_=sgbs_d[l:l + 1, :]), w=["bsrow"], dma=True)
        i0 = tmp_next()
        wraw = tmp[:, i0, :].rearrange("p (g s) -> p g s", g=4)
        op("sp", lambda e: e.dma_start(out=wraw, in_=sgws_d[l].rearrange("g t s -> t g s")), w=[("tmp", i0)], dma=True, semkey="wsld")
        b = bank("b")
        for g in range(4):
            tp(psum[b][:, g * 128:(g + 1) * 128], wraw[:, g, :], r=[("tmp", i0)], w=[PK(b)])
        tt(wsT[:], psum[b][:].rearrange("p (g t) -> p g t", g=4), cst[:, 6:10, :], ALU.mult, r=[PK(b), "cst"], w=["wsT"])

    def proj_fm(iw, col, tb, b):
        sl = slice(tb * TB, (tb + 1) * TB)
        for kc in range(KC):
            mm(psum[b][:], wk[iw][:, kc, col:col + 128], xb[:, kc, sl], kc == 0, kc == KC - 1,
               r=[("wk", iw), ("xb", kc, tb)], w=[PK(b)])

    def proj_tm(iw0, iw1, tb, t4, b):
        c0 = tb * TB + t4 * 128
        for half, iw in enumerate((iw0, iw1)):
            for kc in range(KC):
                mm(psum[b][:, half * 256:(half + 1) * 256], xb[:, kc, c0:c0 + 128], wk[iw][:, kc, :], kc == 0, kc == KC - 1,
                   r=[("wk", iw), ("xb", kc, tb)], w=[PK(b)])

    def mixer(l, hf):
        win = win_d[l].rearrange("(kc p) n -> p kc n", p=128)
        wa = wa_d[l].rearrange("(kc p) n -> p kc n", p=128)
        wbb = wbb_d[l].rearrange("(kc p) n -> p kc n", p=128)
        wo = wo_d[l].rearrange("(kc p) n -> p kc n", p=128)
        mixer_setup(l)

        def load_in(col0):
            i0, i1 = wk_next(), wk_next()
            load_wk(i0, win[:, :, col0:col0 + 256])
            load_wk(i1, win[:, :, col0 + 256:col0 + 512])
            return i0, i1

        for tb in range(2):
            sl = slice(tb * TB, (tb + 1) * TB)
            wg = load_in(1536)
            for h in range(4):
                b = bank("a6")
                proj_fm(wg[h // 2], (h % 2) * 128, tb, b)
                it_ = tmp_next()
                act(tmp[:, it_, :], psum[b][:], AF.Silu, r=[PK(b)], w=[("tmp", it_)])
                ts(sgT[:, h, :], tmp[:, it_, :], hcol[:, 16 + l * 4 + h:16 + l * 4 + h + 1], None, ALU.mult, None,
                   r=[("tmp", it_), "cols"], w=sgT_k)
                drain(1)
            wu = load_in(2048)
            for g in range(4):
                b = bank("a6")
                proj_fm(wu[g // 2], (g % 2) * 128, tb, b)
                gelu_from_psum(b, guT[:, g, :], guT_k)
                drain(1)
            wv_ = load_in(2560)
            for t4 in range(4):
                b = bank("a6")
                proj_tm(wv_[0], wv_[1], tb, t4, b)
                gelu_from_psum(b, gv[:, t4, :], gv_k)
                op("dve", lambda e, t4=t4: e.bn_stats(out=small[:, t4 * 6:(t4 + 1) * 6], in_=gv[:, t4, :]), r=gv_k, w=["small_bn"])
                op("dve", lambda e, t4=t4: e.bn_aggr(out=small[:, 24 + t4 * 2:24 + t4 * 2 + 2], in_=small[:, t4 * 6:(t4 + 1) * 6]),
                   r=["small_bn"], w=["small_mv"])
            drain()
            mvv = small[:, 24:32].rearrange("p (t two) -> p t two", two=2)
            act(small[:, 32:36], mvv[:, :, 1], AF.Ln, r=["small_mv", "epsc"], w=["small_rs"], bias=epsc[:, 0:1], scale=1.0)
            act(small[:, 32:36], small[:, 32:36], AF.Exp, r=["small_rs"], w=["small_rs"], scale=-0.5)
            stt(small[:, 36:40], mvv[:, :, 0], -1.0, small[:, 32:36], ALU.mult, ALU.mult, r=["small_mv", "small_rs"], w=["small_rs"])
            for t4 in range(4):
                it_ = tmp_next()
                t = tmp[:, it_, :]
                act(t, gv[:, t4, :], AF.Identity, r=gv_k + ["small_rs"], w=[("tmp", it_)],
                    scale=small[:, 32 + t4:33 + t4], bias=small[:, 36 + t4:37 + t4])
                tt(t, t, rows[:, 2, :], ALU.mult, r=[("tmp", it_), "rows2"], w=[("tmp", it_)])
                tt(vn[:, t4, :], t, rows[:, 3, :], ALU.add, r=[("tmp", it_), "rows3"], w=vn_k)
            wf = load_in(512)
            fk = []
            for t4 in range(4):
                b = bank("a")
                proj_tm(wf[0], wf[1], tb, t4, b)
                ie, ik = tmp_next(), tmp_next()
                e_, k_ = tmp[:, ie, :], tmp[:, ik, :]
                ke, kk = ("tmp", ie), ("tmp", ik)
                act(e_, psum[b][:], AF.Exp, r=[PK(b)], w=[ke], scale=-1.0)
                act(e_, e_, AF.Ln, r=[ke, "epsc"], w=[ke], bias=epsc[:, 2:3], scale=1.0)
                act(e_, e_, AF.Exp, r=[ke], w=[ke], scale=-1.0)
                tt(e_, e_, rows[:, 1, :], ALU.mult, r=[ke, "rows1"], w=[ke])
                tt(e_, e_, rows[:, 0, :], ALU.add, r=[ke, "rows01"], w=[ke])
                act(lf[:, t4, :], e_, AF.Ln, r=[ke], w=lf_k)
                ts(kh[:, t4, :], e_, -1.0, 1.0, ALU.mult, ALU.add, r=[ke], w=kh_k)
            wi = load_in(1024)
            for t4 in range(4):
                b = bank("a")
                proj_tm(wi[0], wi[1], tb, t4, b)
                act(it[:, t4, :], psum[b][:], AF.Identity, r=[PK(b)], w=it_k)
            for t4 in range(4):
                b2 = bank("b")
                mm(psum[b2][:], cst[:, 1, :], lf[:, t4, :], True, True, r=["cst"] + lf_k, w=[PK(b2)])
                ie = tmp_next()
                e_ = tmp[:, ie, :]
                ke = ("tmp", ie)
                act(e_, psum[b2][:], AF.Exp, r=[PK(b2)], w=[ke])
                tt(kh[:, t4, :], kh[:, t4, :], e_, ALU.mult, r=kh_k + [ke], w=kh_k)
            wq = load_in(0)
            for h in range(4):
                b = bank("a")
                proj_fm(wf[h // 2], (h % 2) * 128, tb, b)
                ie, ibt, ieq, iek = tmp_next(), tmp_next(), tmp_next(), tmp_next()
                e_, bTs, eq, ek = tmp[:, ie, :], tmp[:, ibt, :], tmp[:, ieq, :], tmp[:, iek, :]
                ke, kbt, keq, kek = ("tmp", ie), ("tmp", ibt), ("tmp", ieq), ("tmp", iek)
                act(e_, psum[b][:], AF.Exp, r=[PK(b)], w=[ke], scale=-1.0)
                act(e_, e_, AF.Ln, r=[ke, "epsc"], w=[ke], bias=epsc[:, 2:3], scale=1.0)
                act(e_, e_, AF.Exp, r=[ke], w=[ke], scale=-1.0)
                ts(e_, e_, lbc[:, 3, l, h:h + 1], lbc[:, 2, l, h:h + 1], ALU.mult, ALU.add, r=[ke, "lbc"], w=[ke])
                b2 = bank("b")
                for t4 in range(4):
                    mm(psum[b2][:, t4 * 128:(t4 + 1) * 128], lf[:, t4, h * 128:(h + 1) * 128], cst[:, 2, :], True, True,
                       r=lf_k + ["cst"], w=[PK(b2)])
                act(bTs, psum[b2][:], AF.Identity, r=[PK(b2)], w=[kbt])
                dk = ("dec", h)
                bT3 = bTs.rearrange("p (c n) -> p c n", n=64)
                ts(dec[:, h, 0, :], bT3[:, :, 31], -1.0, None, ALU.mult, None, r=[kbt], w=[dk])
                act(dec[:, h, 1, :], bT3[:, :, 31], AF.Exp, r=[kbt], w=[dk])
                act(dec[:, h, 2, :], bT3[:, :, 63], AF.Exp, r=[kbt], w=[dk])
                tt(eq.rearrange("p (c n) -> p c n", n=64), bT3, dec[:, h, 0, :].unsqueeze(2).to_broadcast([128, 8, 64]), ALU.add,
                   r=[kbt, dk], w=[keq])
                act(ek, eq, AF.Exp, r=[keq], w=[kek], scale=-1.0)
                act(eq, eq, AF.Exp, r=[keq], w=[keq])
                tt(ktT[:, h, :], e_, ek, ALU.mult, r=[ke, kek], w=[("ktT", h)])
                b3 = bank("a")
                proj_fm(wq[h // 2], (h % 2) * 128, tb, b3)
                tt(qtT[:, h, :], psum[b3][:], eq, ALU.mult, r=[PK(b3), keq], w=[("qtT", h)])
            for t4 in range(4):
                ts_ = slice(t4 * 128, (t4 + 1) * 128)
                bp = bank("b")
                for h in range(4):
                    mm(psum[bp][:, h * 128:(h + 1) * 128], ktT[:, h, ts_], qtT[:, h, ts_], True, True,
                       r=[("ktT", h), ("qtT", h)], w=[PK(bp)])
                ip = t4 % 2
                tt(PTs[:, ip, :].rearrange("p (h t) -> p h t", h=4), psum[bp][:].rearrange("p (h t) -> p h t", h=4), cst[:, 2:6, :],
                   ALU.mult, r=[PK(bp), "cst"], w=[("PTs", ip)])
                bo = 6
                for h in range(4):
                    dk = ("dec", h)
                    Sh = Sst[:, l, h, :]
                    sk = ("S", l, h)
                    ck0 = 2 * t4
                    hs = slice(h * 128, (h + 1) * 128)
                    ts(Smb[:, 0, h, :], Sh, dec[:, h, 1, ck0:ck0 + 1], None, ALU.mult, None, r=[sk, dk], w=[("Smb", 0, h)])
                    mm(psum[bo][:, hs], it[:, t4, hs], PTs[:, ip, hs], True, False, r=it_k + [("PTs", ip)], w=[PK(bo)], sgc=True)
                    mm(psum[bo][:, h * 128:h * 128 + 64], Smb[:, 0, h, :], qtT[:, h, t4 * 128:t4 * 128 + 64], False, False,
                       r=[("Smb", 0, h), ("qtT", h)], w=[PK(bo)], sgc=True)
                    bs_ = bank("a")
                    mm(psum[bs_][:, 0:128], kh[0:64, t4, hs], it[0:64, t4, hs], True, True, r=kh_k + it_k, w=[PK(bs_)])
                    stt(Sh, Sh, dec[:, h, 2, ck0:ck0 + 1], psum[bs_][:, 0:128], ALU.mult, ALU.add, r=[sk, dk, PK(bs_)], w=[sk])
                    ts(Smb[:, 1, h, :], Sh, dec[:, h, 1, ck0 + 1:ck0 + 2], None, ALU.mult, None, r=[sk, dk], w=[("Smb", 1, h)])
                    mm(psum[bo][:, h * 128 + 64:h * 128 + 128], Smb[:, 1, h, :], qtT[:, h, t4 * 128 + 64:t4 * 128 + 128], False, True,
                       r=[("Smb", 1, h), ("qtT", h)], w=[PK(bo)], sgc=True)
                    bs2 = bank("a")
                    mm(psum[bs2][:, 0:128], kh[64:128, t4, hs], it[64:128, t4, hs], True, True, r=kh_k + it_k, w=[PK(bs2)])
                    stt(Sh, Sh, dec[:, h, 2, ck0 + 1:ck0 + 2], psum[bs2][:, 0:128], ALU.mult, ALU.add, r=[sk, dk, PK(bs2)], w=[sk])
                iq = tbf_next()
                act(tbf[:, iq, :], psum[bo][:], AF.Square, r=[PK(bo)], w=[("tbf", iq)])
                mm(psum[7][:], ones_bf[:], tbf[:, iq, :], True, True, r=["ones_bf", ("tbf", iq)], w=[PK(7)])
                ir = tmp_next()
                rs = tmp[:, ir, :]
                act(rs, psum[7][:], AF.Ln, r=[PK(7), "epsc"], w=[("tmp", ir)], bias=epsc[:, 1:2], scale=1.0 / 128)
                act(rs, rs, AF.Exp, r=[("tmp", ir)], w=[("tmp", ir)], scale=-0.5)
                tt(rs, psum[bo][:], rs, ALU.mult, r=[PK(bo), ("tmp", ir)], w=[("tmp", ir)])
                tt(ogT[:, :, ts_], rs.rearrange("p (h t) -> p h t", h=4), sgT[:, :, ts_], ALU.mult, r=[("tmp", ir)] + sgT_k, w=og_k)
            for g in range(4):
                b = bank("b")
                for t4 in range(4):
                    ts_ = slice(t4 * 128, (t4 + 1) * 128)
                    mm(psum[b][:, ts_], vn[:, t4, g * 128:(g + 1) * 128], wsT[:, g, :], True, False, r=vn_k + ["wsT"], w=[PK(b)])
                    mm(psum[b][:, ts_], ones_f[0:1, :], bsrow[0:1, g * 128:(g + 1) * 128], False, True, r=["ones_f", "bsrow"], w=[PK(b)])
                tt(hbT[:, g, :], psum[b][:], guT[:, g, :], ALU.mult, r=[PK(b)] + guT_k, w=hb_k)
            v5 = lambda t: t[:].rearrange("p k n -> p (k n)").rearrange("p (a k n) -> p a k n", a=2, k=4)
            for c2 in range(4):
                iab = wk_next()
                load_wk(iab, wa[:, :, c2 * 256:(c2 + 1) * 256], view=lambda t: v5(t)[:, 0])
                load_wk(iab, wbb[:, :, c2 * 256:(c2 + 1) * 256], view=lambda t: v5(t)[:, 1])
                iga, igb = wk_next(), wk_next()
                load_wk(iga, win[:, :, 3072 + c2 * 256:3072 + (c2 + 1) * 256])
                load_wk(igb, win[:, :, 4096 + c2 * 256:4096 + (c2 + 1) * 256])
                for cc in range(2):
                    c = 2 * c2 + cc
                    pya, pza, pyb, pzb = (0, 1, 2, 3) if c % 2 == 0 else (4, 5, 6, 7)
                    wav = v5(wk[iab])[:, 0]
                    wbv = v5(wk[iab])[:, 1]
                    co = cc * 128
                    for k4 in range(4):
                        mm(psum[pya][:], wav[:, k4, co:co + 128], ogT[:, k4, :], k4 == 0, k4 == 3, r=[("wk", iab)] + og_k, w=[PK(pya)])
                    proj_fm(iga, cc * 128, tb, pza)
                    for k4 in range(4):
                        mm(psum[pyb][:], wbv[:, k4, co:co + 128], hbT[:, k4, :], k4 == 0, k4 == 3, r=[("wk", iab)] + hb_k, w=[PK(pyb)])
                    proj_fm(igb, cc * 128, tb, pzb)
                    i1, i2 = tmp_next(), tmp_next()
                    t1, t2 = tmp[:, i1, :], tmp[:, i2, :]
                    act(t1, psum[pza][:], AF.Tanh, r=[PK(pza)], w=[("tmp", i1)], scale=0.5)
                    act(t2, psum[pzb][:], AF.Tanh, r=[PK(pzb)], w=[("tmp", i2)], scale=0.5)
                    stt(t1, t1, 1.0, psum[pya][:], ALU.add, ALU.mult, r=[("tmp", i1), PK(pya)], w=[("tmp", i1)])
                    stt(t2, t2, 1.0, psum[pyb][:], ALU.add, ALU.mult, r=[("tmp", i2), PK(pyb)], w=[("tmp", i2)])
                    tt(yT[:, c, :], t1, t2, ALU.add, r=[("tmp", i1), ("tmp", i2)], w=[yT_k[c]])
            for c2 in range(4):
                iw = wk_next()
                load_wk(iw, wo[:, :, c2 * 256:(c2 + 1) * 256])
                for cc in range(2):
                    c = 2 * c2 + cc
                    b = bank("b")
                    for kc in range(KC):
                        mm(psum[b][:], wk[iw][:, kc, cc * 128:(cc + 1) * 128], yT[:, kc, :], kc == 0, kc == KC - 1,
                           r=[("wk", iw), yT_k[kc]], w=[PK(b)])
                    flush_stats()
                    resid_chunk(c, tb, b, c == 0, c == KC - 1)
            drain()
            ln_begin(tb, (l * 3 + 1) * 8, False)

    dec = sb("dec", [128, 4, 3, 8], F32)

    def load_x(hf):
        xin = x_d[hf * NT:(hf + 1) * NT, :].rearrange("(t p) d -> p t d", p=128)
        for t8 in range(8):
            st_ap, st_k = scr_f32(t8 * 4, 4)
            op("sp", lambda e, st_ap=st_ap, t8=t8: e.dma_start(out=st_ap, in_=xin[:, t8, :]), w=st_k, dma=True, semkey=("xst", t8))
        for c in range(KC):
            for tb in range(2):
                b = bank("a")
                rk = []
                for i in range(4):
                    t8 = tb * 4 + i
                    st_ap, st_k = scr_f32(t8 * 4, 4)
                    tp(psum[b][:, i * 128:(i + 1) * 128], st_ap[:, c * 128:(c + 1) * 128], r=st_k, w=[PK(b)])
                sl = slice(tb * TB, (tb + 1) * TB)
                act(xa[:, c, sl], psum[b][:], AF.Identity, r=[PK(b)], w=[("xa", c, tb)], scale=ALPHA)
                op("dve", lambda e, c=c, sl=sl, b=b: e.tensor_copy(out=xb[:, c, sl], in_=psum[b][:]), r=[PK(b), ("xa", c, tb)], w=[("xb", c, tb)])

    def store_x(hf):
        xout = out_d[hf * NT:(hf + 1) * NT, :].rearrange("(t p) d -> p t d", p=128)
        for t8 in range(8):
            tb, i = t8 // 4, t8 % 4
            st_ap, st_k = scr_f32(t8 * 4, 4)
            for half in range(2):
                b = bank("a")
                for cc in range(4):
                    c = half * 4 + cc
                    tp(psum[b][:, cc * 128:(cc + 1) * 128], xa[:, c, t8 * 128:(t8 + 1) * 128], r=[("xa", c, tb)], w=[PK(b)])
                if half == 0:
                    act(st_ap[:, 0:512], psum[b][:], AF.Identity, r=[PK(b)], w=st_k)
                else:
                    op("dve", lambda e, st_ap=st_ap, b=b: e.tensor_copy(out=st_ap[:, 512:1024], in_=psum[b][:]), r=[PK(b)], w=st_k)
            op("sp", lambda e, st_ap=st_ap, t8=t8: e.dma_start(out=xout[:, t8, :], in_=st_ap), r=st_k, w=[("out", hf, t8)], dma=True,
               semkey=("ost", t8))

    def dbg_dump(i):
        if debug:
            drain()
            op("sp", lambda e, i=i: e.dma_start(out=dbg_d[i], in_=xa[:]), r=[("xa", c, tb) for c in range(KC) for tb in range(2)],
               w=[("dbg", i)], dma=True, semkey=("dbg", i))

    for hf in range(nhalf):
        load_x(hf)
        for l in range(depth):
            last = (l == depth - 1)
            if STAGES >= 1:
                ffn(l, 0, l * 3 + 0, False)
            if hf == 0 and l == 0:
                dbg_dump(0)
            if STAGES >= 2:
                mixer(l, hf)
            if hf == 0 and l == 0:
                dbg_dump(1)
            if STAGES >= 3:
                ffn(l, 1, l * 3 + 2, last)
            if hf == 0 and l == 0:
                dbg_dump(2)
        drain()
        store_x(hf)
    S.emit()
    es.close()
    return nc


_CACHE = {}


def kernel(x, ffn1_w13, ffn1_w2, ffn2_w13, ffn2_w2, ln_g, ln_b, w_in, hg_lb, hg_norm_g,
           sg_ln_g, sg_ln_b, sg_ws, sg_bs, w_branch_a, w_branch_b, w_out):
    f = lambda a: np.ascontiguousarray(np.asarray(a, dtype=np.float32))
    x = f(x)
    B = x.shape[0]
    if "nc" not in _CACHE:
        _CACHE["nc"] = build_program(DEPTH, 2, False)
    nc = _CACHE["nc"]
    shared = dict(
        ffn1_w13=f(ffn1_w13), ffn1_w2=f(ffn1_w2), ffn2_w13=f(ffn2_w13), ffn2_w2=f(ffn2_w2),
        ln_g=f(ln_g).reshape(DEPTH * 3, D), ln_b=f(ln_b).reshape(DEPTH * 3, D), w_in=f(w_in),
        hg_lb=f(hg_lb), hg_norm_g=f(hg_norm_g), sg_ln_g=f(sg_ln_g), sg_ln_b=f(sg_ln_b),
        sg_ws=f(sg_ws), sg_bs=f(sg_bs).reshape(DEPTH, 512), w_branch_a=f(w_branch_a),
        w_branch_b=f(w_branch_b), w_out=f(w_out), consts=make_consts())
    in_maps = [dict(shared, x=x[b]) for b in range(B)]
    res = run_bass_kernel_spmd(nc, in_maps, core_ids=list(range(B)))
    return np.stack([np.asarray(r["out"], dtype=np.float32) for r in res.results], axis=0)
```
